# Optimizing a Trainium2 kernel written in Bass

```python
import math
import jax, jax.numpy as jnp
from jax import lax
import numpy as np

D_MODEL = 2048
BATCH = 4
SEQ = 2048
DEPTH = 1

D_MIX = D_MODEL
HEAD_DIM = 64
D_ATTN = D_MIX // 2
N_Q_HEADS = D_ATTN // HEAD_DIM
N_KV_HEADS = 2
Q_PER_KV = N_Q_HEADS // N_KV_HEADS
D_KV = N_KV_HEADS * HEAD_DIM
WINDOW = 128
BLOCK = WINDOW
D_SGU = D_MIX - D_ATTN
SGU_GROUPS = 8
SGU_GROUP_DIM = D_SGU // SGU_GROUPS
CHUNK = 128
D_IN = D_ATTN + 2 * D_KV + D_ATTN + 3 * D_SGU
EPS = 1e-6

kernel_name = "hybrid_swa_sink_gmlp_parallel_heads"


def rms_norm(x, g):
    xf = x.astype(jnp.float32)
    y = xf * lax.rsqrt(jnp.mean(xf * xf, axis=-1, keepdims=True) + EPS)
    return (y * g.astype(jnp.float32)).astype(x.dtype)


def layer_norm(x, g, b):
    xf = x.astype(jnp.float32)
    mu = jnp.mean(xf, axis=-1, keepdims=True)
    var = jnp.mean(jnp.square(xf - mu), axis=-1, keepdims=True)
    y = (xf - mu) * lax.rsqrt(var + EPS)
    return (y * g.astype(jnp.float32) + b.astype(jnp.float32)).astype(x.dtype)


def sliding_window_sink_attention(q, k, v, sinks):
    B, S = q.shape[0], q.shape[1]
    nb = S // BLOCK
    qb = q.reshape(B, nb, BLOCK, N_KV_HEADS, Q_PER_KV, HEAD_DIM)
    kb = k.reshape(B, nb, BLOCK, N_KV_HEADS, HEAD_DIM)
    vb = v.reshape(B, nb, BLOCK, N_KV_HEADS, HEAD_DIM)
    pad = ((0, 0), (1, 0), (0, 0), (0, 0), (0, 0))
    k_ext = jnp.concatenate([jnp.pad(kb, pad)[:, :-1], kb], axis=2)
    v_ext = jnp.concatenate([jnp.pad(vb, pad)[:, :-1], vb], axis=2)
    scale = 1.0 / math.sqrt(HEAD_DIM)
    scores = jnp.einsum('bnqhgd,bnshd->bnhgqs', qb, k_ext).astype(jnp.float32) * scale
    qpos = jnp.arange(BLOCK)[:, None] + BLOCK
    kpos = jnp.arange(2 * BLOCK)[None, :]
    dist = qpos - kpos
    band = (dist >= 0) & (dist < WINDOW)
    has_prev = (jnp.arange(nb) > 0)[:, None, None] | (kpos >= BLOCK)[None]
    valid = band[None] & has_prev
    scores = jnp.where(valid[None, :, None, None], scores, -jnp.inf)
    sink = sinks.astype(jnp.float32).reshape(N_KV_HEADS, Q_PER_KV)
    sink = jnp.broadcast_to(sink[None, None, :, :, None, None], scores.shape[:-1] + (1,))
    probs = jax.nn.softmax(jnp.concatenate([scores, sink], axis=-1), axis=-1)[..., :-1]
    out = jnp.einsum('bnhgqs,bnshd->bnqhgd', probs.astype(v.dtype), v_ext)
    return out.reshape(B, S, N_Q_HEADS * HEAD_DIM)


def chunked_spatial_gating(u, v, ln_g, ln_b, w_s, b_s):
    B, S = u.shape[0], u.shape[1]
    nc = S // CHUNK
    vn = layer_norm(v, ln_g, ln_b).reshape(B, nc, CHUNK, SGU_GROUPS, SGU_GROUP_DIM)
    causal = jnp.tril(jnp.ones((CHUNK, CHUNK), dtype=w_s.dtype))
    w = w_s * causal[None]
    mixed = jnp.einsum('gts,bnsgc->bntgc', w, vn) + b_s.T[None, None, :, :, None]
    return u * mixed.reshape(B, S, D_SGU)


def setup_inputs(seed: int = 0) -> dict:
    key = jax.random.key(seed)
    ks = jax.random.split(key, 14)
    f32 = jnp.float32
    x = jax.random.normal(ks[0], (BATCH, SEQ, D_MODEL), f32)
    c = jax.random.normal(ks[1], (BATCH, D_MODEL), f32)
    norm_g = 1.0 + 0.05 * jax.random.normal(ks[2], (DEPTH, D_MODEL), f32)
    w_ada = 0.5 * D_MODEL ** -0.5 * jax.random.normal(ks[3], (DEPTH, D_MODEL, 3 * D_MODEL), f32)
    b_ada = 0.02 * jax.random.normal(ks[4], (DEPTH, 3 * D_MODEL), f32)
    w_in = D_MODEL ** -0.5 * jax.random.normal(ks[5], (DEPTH, D_MODEL, D_IN), f32)
    attn_sinks = 0.5 * jax.random.normal(ks[6], (DEPTH, N_Q_HEADS), f32)
    sgu_ln_g = 1.0 + 0.05 * jax.random.normal(ks[7], (DEPTH, D_SGU), f32)
    sgu_ln_b = 0.02 * jax.random.normal(ks[8], (DEPTH, D_SGU), f32)
    sgu_w = CHUNK ** -0.5 * jax.random.normal(ks[9], (DEPTH, SGU_GROUPS, CHUNK, CHUNK), f32)
    sgu_b = 1.0 + 0.1 * jax.random.normal(ks[10], (DEPTH, SGU_GROUPS, CHUNK), f32)
    w_out = D_MIX ** -0.5 * jax.random.normal(ks[11], (DEPTH, D_MIX, D_MODEL), f32)
    final_g = 1.0 + 0.05 * jax.random.normal(ks[12], (D_MODEL,), f32)
    return {"x": x, "c": c, "norm_g": norm_g, "w_ada": w_ada, "b_ada": b_ada,
            "w_in": w_in, "attn_sinks": attn_sinks, "sgu_ln_g": sgu_ln_g,
            "sgu_ln_b": sgu_ln_b, "sgu_w": sgu_w, "sgu_b": sgu_b, "w_out": w_out,
            "final_g": final_g}


def reference(x, c, norm_g, w_ada, b_ada, w_in, attn_sinks, sgu_ln_g, sgu_ln_b,
              sgu_w, sgu_b, w_out, final_g):
    B, S = x.shape[0], x.shape[1]
    splits = np.cumsum([D_ATTN, D_KV, D_KV, D_ATTN, D_SGU, D_SGU])
    c_act = jax.nn.silu(c)
    for l in range(DEPTH):
        mod = c_act @ w_ada[l] + b_ada[l]
        shift, scale, gate = jnp.split(mod, 3, axis=-1)
        h = rms_norm(x, norm_g[l]) * (1.0 + scale[:, None, :]) + shift[:, None, :]
        z = h @ w_in[l]
        q, k, v, g_attn, u, v_s, g_sgu = jnp.split(z, splits, axis=-1)
        attn = sliding_window_sink_attention(
            q.reshape(B, S, N_Q_HEADS, HEAD_DIM),
            k.reshape(B, S, N_KV_HEADS, HEAD_DIM),
            v.reshape(B, S, N_KV_HEADS, HEAD_DIM),
            attn_sinks[l]) * jax.nn.silu(g_attn)
        sgu = chunked_spatial_gating(u, v_s, sgu_ln_g[l], sgu_ln_b[l],
                                     sgu_w[l], sgu_b[l]) * jax.nn.silu(g_sgu)
        y = jnp.concatenate([attn, sgu], axis=-1) @ w_out[l]
        x = x + gate[:, None, :] * y
    return rms_norm(x, final_g)
```

```python
import numpy as np
import concourse.bass as bass
import concourse.mybir as mybir
from concourse.bass_utils import run_bass_kernel_spmd

F32 = mybir.dt.float32
BF16 = mybir.dt.bfloat16
ALU = mybir.AluOpType
AF = mybir.ActivationFunctionType

D = 2048
KC = 16
T = 1024
TH = T + 128
NB = 8
D_IN = 5376
EPS = 1e-6
NCORES = 8


class V:
    __slots__ = ("ap", "rng")

    def __init__(self, ap, rng):
        self.ap = ap
        self.rng = rng


class LT:
    def __init__(self, key, base_ap, byte_off, dims, dtype):
        self.key = key
        self.off = byte_off
        self.dims = list(dims)
        self.ds = 2 if dtype == BF16 else 4
        if len(dims) == 1:
            self.ap = base_ap
        elif len(dims) == 2:
            self.ap = base_ap.rearrange("p (a b) -> p a b", a=dims[0])
        elif len(dims) == 3:
            self.ap = base_ap.rearrange("p (a b c) -> p a b c", a=dims[0], b=dims[1])
        elif len(dims) == 4:
            self.ap = base_ap.rearrange("p (a b c d) -> p a b c d", a=dims[0], b=dims[1], c=dims[2])
        else:
            raise ValueError

    def v(self, *idx, p=None):
        idx = list(idx) + [None] * (len(self.dims) - len(idx))
        key = [slice(None) if p is None else slice(p[0], p[1])]
        norm = []
        for i, d in zip(idx, self.dims):
            if i is None:
                key.append(slice(None)); norm.append((0, d))
            elif isinstance(i, int):
                key.append(i); norm.append((i, i + 1))
            else:
                key.append(slice(i[0], i[1])); norm.append((i[0], i[1]))
        ap = self.ap[tuple(key)]
        n = len(self.dims)
        strides = [1] * n
        for k in range(n - 2, -1, -1):
            strides[k] = strides[k + 1] * self.dims[k + 1]
        k = n - 1
        while k > 0 and norm[k] == (0, self.dims[k]):
            k -= 1
        outer = [range(a, b) for (a, b) in norm[:k]]
        cnt = 1
        for r in outer:
            cnt *= len(r)
        rng = []
        if cnt > 48:
            lo = sum(norm[j][0] * strides[j] for j in range(n))
            hi = sum((norm[j][1] - 1) * strides[j] for j in range(n)) + 1
            rng.append((self.key, self.off + lo * self.ds, self.off + hi * self.ds))
        else:
            import itertools
            for combo in itertools.product(*outer):
                base = sum(c * strides[j] for j, c in enumerate(combo))
                lo = base + norm[k][0] * strides[k]
                hi = base + norm[k][1] * strides[k]
                rng.append((self.key, self.off + lo * self.ds, self.off + hi * self.ds))
        return V(ap, rng)


class Prog:
    ENG = ("pe", "act", "dve", "pool", "sp")

    def __init__(self, nc):
        self.nc = nc
        self.q = {e: [] for e in self.ENG}
        self.sems = {}
        self.cnt = {}
        for e in ("pe", "act", "dve", "pool"):
            self.sems[e] = nc.alloc_semaphore("s_" + e)
            self.cnt[e] = 0
        self.seen = {e: {} for e in self.ENG}
        self.recs = {}
        self.pe_pending = False
        self.final_waits = []

    def _deps(self, reads, writes, eng=None):
        waits = []
        for v in reads:
            for (key, lo, hi) in v.rng:
                for r in self.recs.get(key, ()):
                    if (r[3] or (key == "psum" and r[4] != eng)) and r[0] < hi and lo < r[1]:
                        waits.append(r[2])
        for v in writes:
            for (key, lo, hi) in v.rng:
                for r in self.recs.get(key, ()):
                    if r[0] < hi and lo < r[1]:
                        waits.append(r[2])
        return waits

    def _record(self, eng, ticket, reads, writes):
        for v in writes:
            for (key, lo, hi) in v.rng:
                lst = self.recs.setdefault(key, [])
                lst[:] = [r for r in lst if not (lo <= r[0] and r[1] <= hi)]
                lst.append((lo, hi, ticket, True, eng))
        for v in reads:
            for (key, lo, hi) in v.rng:
                lst = self.recs.setdefault(key, [])
                for i, r in enumerate(lst):
                    if (not r[3]) and r[4] == eng and r[0] == lo and r[1] == hi:
                        lst[i] = (lo, hi, ticket, False, eng)
                        break
                else:
                    lst.append((lo, hi, ticket, False, eng))

    def _emit_waits(self, eng, waits):
        best = {}
        for (sk, val, src) in waits:
            if src == "pe" and eng == "pe":
                continue
            if val > best.get(sk, 0):
                best[sk] = val
        for sk, val in best.items():
            if self.seen[eng].get(sk, 0) >= val:
                continue
            self.seen[eng][sk] = val
            self.q[eng].append(("w", sk, val))

    def op(self, eng, fn, reads=(), writes=(), signal=True):
        self._emit_waits(eng, self._deps(reads, writes, eng))
        if signal:
            self.cnt[eng] += 1
            ticket = (eng, self.cnt[eng], eng)
            self.q[eng].append(("i", fn, eng, 1))
            if eng == "pe":
                self.pe_pending = False
        else:
            assert eng == "pe"
            ticket = (eng, self.cnt[eng] + 1, eng)
            self.q[eng].append(("i", fn, None, 0))
            self.pe_pending = True
        self._record(eng, ticket, reads, writes)
        return ticket

    def dma(self, eng, semkey, out_ap, in_ap, reads=(), writes=(), final=False, noncontig=False):
        if semkey not in self.sems:
            self.sems[semkey] = self.nc.alloc_semaphore("d_" + semkey)
            self.cnt[semkey] = 0
        self._emit_waits(eng, self._deps(reads, writes))
        self.cnt[semkey] += 16
        ticket = (semkey, self.cnt[semkey], "dma")
        if noncontig:
            def fn(e, o=out_ap, i=in_ap):
                with self.nc.allow_non_contiguous_dma(reason="small strided param load"):
                    return e.dma_start(out=o, in_=i)
        else:
            def fn(e, o=out_ap, i=in_ap):
                return e.dma_start(out=o, in_=i)
        self.q[eng].append(("i", fn, semkey, 16))
        self._record("dma", ticket, reads, writes)
        if final:
            self.final_waits.append(ticket)
        return ticket

    def finish(self):
        assert not self.pe_pending
        self._emit_waits("sp", self.final_waits)
        nc = self.nc
        sems = self.sems
        q = self.q

        def run(e, items):
            for it in items:
                if it[0] == "w":
                    e.wait_ge(sems[it[1]], it[2])
                else:
                    ins = it[1](e)
                    if it[2] is not None:
                        ins.then_inc(sems[it[2]], it[3])

        with nc.Block() as block:
            @block.sync
            def _(e):
                run(e, q["sp"])

            @block.gpsimd
            def _(e):
                run(e, q["pool"])

            @block.scalar
            def _(e):
                run(e, q["act"])

            @block.vector
            def _(e):
                run(e, q["dve"])

            @block.tensor
            def _(e):
                run(e, q["pe"])


class _Stop(Exception):
    pass


def build_nc(stop=None):
    nc = bass.Bass("TRN2", target_bir_lowering=False)
    P = Prog(nc)
    try:
        _build_body(nc, P, stop)
    except _Stop:
        pass
    P.finish()
    return nc


def _build_body(nc, P, stop):
    def maybe_stop(tag):
        if stop == tag:
            raise _Stop()

    def din(name, shape):
        return nc.dram_tensor(name, list(shape), F32, kind="ExternalInput").ap()

    x_d = din("x", [TH, D])
    c_d = din("c", [D])
    ng_d = din("norm_g", [D])
    wada_d = din("w_ada", [D, 3 * D])
    bada_d = din("b_ada", [3 * D])
    win_d = din("w_in", [D, D_IN])
    sink_d = din("sinks", [16])
    lng_d = din("ln_g", [1024])
    lnb_d = din("ln_b", [1024])
    sw_d = din("sgu_w", [8, 128, 128])
    sb_d = din("sgu_b", [1024])
    wout_d = din("w_out", [D, D])
    fg_d = din("final_g", [D])
    mask_d = din("masks", [128, 3 * 128])
    y_d = nc.dram_tensor("y", [T, D], F32, kind="ExternalOutput").ap()

    def arena(name, nbytes):
        return nc.alloc_sbuf_tensor(name, [128, nbytes // 4], F32)

    def lt_in(ar, key, off, dims, dtype):
        n = 1
        for d_ in dims:
            n *= d_
        ds = 2 if dtype == BF16 else 4
        base = ar[:, off // 4:(off + n * ds) // 4]
        if dtype == BF16:
            base = base.bitcast(BF16)
        return LT(key, base, off, dims, dtype)

    def own(name, dims, dtype, parts=128):
        n = 1
        for d_ in dims:
            n *= d_
        h = nc.alloc_sbuf_tensor("sb_" + name, [parts, n], dtype)
        return LT(name, h[:, :], 0, dims, dtype)

    aT_h = nc.alloc_sbuf_tensor("sb_aT", [128, 16 * T], BF16)
    aT = LT("aT", aT_h[:, :], 0, [16, T], BF16)
    aT_f32 = aT_h[:, :].bitcast(F32)

    def aT_tmp(off, dims):
        n = 1
        for d_ in dims:
            n *= d_
        return LT("aT", aT_f32[:, off // 4:(off + 4 * n) // 4], off, dims, F32)
    ring_ar = arena("ring", 32768)
    ring = [lt_in(ring_ar, "ring", s * 8192, [16, 256], BF16) for s in range(4)]
    wo_big = [lt_in(ring_ar, "ring", s * 16384, [16, 512], BF16) for s in range(2)]
    hT_ar = arena("hTa", 36864)
    hT = lt_in(hT_ar, "hTa", 0, [16, TH], BF16)
    wo_big += [lt_in(hT_ar, "hTa", s * 16384, [16, 512], BF16) for s in range(2)]
    S = arena("S", 71680)

    def s_lt(off, dims, dtype):
        return lt_in(S, "S", off, dims, dtype)

    xs = [s_lt(0, [D], F32), s_lt(8192, [D], F32)]
    xn = [s_lt(16384, [D], BF16), s_lt(20480, [D], BF16)]
    sq = s_lt(24576, [D], BF16)
    Wnat = aT_tmp(0, [8, 128])
    Wtmp = aT_tmp(4096, [8, 128])
    rowtmp = aT_tmp(8192, [3, 1024])
    abig = [lt_in(ring_ar, "ring", 0, [16, 512], BF16), lt_in(ring_ar, "ring", 16384, [16, 512], BF16),
            s_lt(28672, [16, 512], BF16), s_lt(45056, [16, 512], BF16)]
    qT = s_lt(0, [8, T], BF16)
    gateT = s_lt(16384, [8, T], BF16)
    kdup = s_lt(32768, [2, TH], BF16)
    Vpad = s_lt(37376, [9, 2, 2, 128], BF16)
    PT = [s_lt(46592 + i * 2048, [8, 128], BF16) for i in range(4)]
    rden = [s_lt(54784 + i * 2048, [4, 128], F32) for i in range(2)]
    g2 = [s_lt(58880 + i * 2048, [4, 128], F32) for i in range(2)]
    tnhB = [s_lt(62976 + i * 4096, [T], F32) for i in range(2)]
    ugT = s_lt(0, [8, T], BF16)
    vn = s_lt(16384, [8, 1024], BF16)
    lnt = [s_lt(32768 + i * 4096, [1024], F32) for i in range(2)]
    tnhC = [s_lt(40960 + i * 4096, [T], F32) for i in range(2)]
    xd = [s_lt(0, [D], F32), s_lt(8192, [D], F32)]
    od = [s_lt(16384, [D], F32), s_lt(24576, [D], F32)]
    tmpD = [s_lt(32768 + i * 2048, [512], F32) for i in range(2)]
    Rg = s_lt(40960, [16, 128], F32)
    gate_bc = s_lt(49152, [D], F32)
    fg_bc = s_lt(57344, [D], F32)

    ident = own("ident", [128], BF16)
    identf = own("identf", [128], F32)
    onesf = own("onesf", [128], F32)
    masks = own("masks", [3, 128], BF16)
    WT = own("WT", [8, 128], BF16)
    maskb = own("maskb", [3, 4, 128], BF16)
    ones_lh = own("ones_lh", [2, 128], BF16)
    lng_bc = own("lng_bc", [1024], F32)
    lnb_bc = own("lnb_bc", [1024], F32)
    cT = own("cT", [16], F32)
    c16 = own("c16", [128], F32, parts=16)
    b48 = own("b48", [128], F32, parts=48)
    ng16 = own("ng16", [128], F32, parts=16)
    tT = own("tT", [16], F32)
    cactT = own("cactT", [16], BF16)
    badaT = own("badaT", [48], F32)
    ngT = own("ngT", [16], F32)
    modT = own("modT", [48], F32)
    geff = own("geff", [16], F32)
    ssq = own("ssq", [32], F32)
    rtmp = own("rtmp", [32], F32)
    rstd = own("rstd", [32], F32)
    mhalf = own("mhalf", [32], F32)
    lnst = own("lnst", [8, 16], F32)
    sink16 = own("sink16", [16], F32, parts=1)
    es16 = own("es16", [16], F32, parts=1)
    es_hi = own("es_hi", [1024], BF16, parts=1)
    es_lo = own("es_lo", [1024], BF16, parts=1)
    bs_hi = own("bs_hi", [1024], BF16, parts=1)
    bs_lo = own("bs_lo", [1024], BF16, parts=1)
    modrow = [own("modrow0", [512], F32, parts=1), own("modrow1", [512], F32, parts=1)]
    ones1 = own("ones1", [128], BF16, parts=1)
    one11 = own("one11", [1], F32, parts=1)

    pp = [nc.alloc_psum_tensor("pp%d" % i, [128, 1024], F32) for i in range(4)]

    def bank(i, a=0, b=512, p=None):
        ps = slice(None) if p is None else slice(p[0], p[1])
        return V(pp[i // 2][ps, (i % 2) * 512 + a:(i % 2) * 512 + b], [("psum", i, i + 1)])

    def bank3(i, n):
        return V(pp[i // 2][:, (i % 2) * 512:(i % 2 + 1) * 512].rearrange("p (a b) -> p a b", a=n),
                 [("psum", i, i + 1)])

    def pair(i):
        return V(pp[i][:, :], [("psum", 2 * i, 2 * i + 2)])

    def pair_bf(i):
        return V(pp[i][:, :].bitcast(BF16).rearrange("p (c t) -> p c t", t=128),
                 [("psum", 2 * i, 2 * i + 2)])


    def vs_(x):
        return [x] if isinstance(x, V) else []

    def sc_(x):
        return x.ap if isinstance(x, V) else x

    def MM(out, lhs, rhs, start, stop, signal, rhs_ap=None, out_ap=None):
        o = out.ap if out_ap is None else out_ap
        l = lhs.ap
        r = rhs.ap if rhs_ap is None else rhs_ap
        P.op("pe", lambda e: e.matmul(o, lhsT=l, rhs=r, start=start, stop=stop),
             reads=[lhs, rhs], writes=[out], signal=signal)

    def TR(out, in_, idv, signal, out_ap=None):
        o = out.ap if out_ap is None else out_ap
        i = in_.ap
        d = idv.ap
        P.op("pe", lambda e: e.transpose(out=o, in_=i, identity=d), reads=[in_, idv], writes=[out], signal=signal)

    def ACTF(out, in_, func, scale=1.0, accum=None, in_ap=None, out_ap=None, extra_w=()):
        o = out.ap if out_ap is None else out_ap
        i = in_.ap if in_ap is None else in_ap
        kw = {}
        if accum is not None:
            kw["accum_out"] = accum.ap
        sc = sc_(scale)
        P.op("act", lambda e: e.activation(out=o, in_=i, func=func, scale=sc, **kw),
             reads=[in_] + vs_(scale), writes=[out] + vs_(accum) + list(extra_w))

    def TS(eng, out, in0, s1, s2, op0, op1=None, in0_ap=None, out_ap=None):
        o = out.ap if out_ap is None else out_ap
        i = in0.ap if in0_ap is None else in0_ap
        a1, a2 = sc_(s1), sc_(s2)
        kw = {} if op1 is None else {"op1": op1}
        P.op(eng, lambda e: e.tensor_scalar(out=o, in0=i, scalar1=a1, scalar2=a2, op0=op0, **kw),
             reads=[in0] + vs_(s1) + vs_(s2), writes=[out])

    def TT(eng, out, in0, in1, op, in0_ap=None, in1_ap=None, out_ap=None):
        o = out.ap if out_ap is None else out_ap
        i0 = in0.ap if in0_ap is None else in0_ap
        i1 = in1.ap if in1_ap is None else in1_ap
        P.op(eng, lambda e: e.tensor_tensor(out=o, in0=i0, in1=i1, op=op), reads=[in0, in1], writes=[out])

    def STT(out, in0, scalar, in1, op0, op1, in0_ap=None, in1_ap=None):
        o = out.ap
        i0 = in0.ap if in0_ap is None else in0_ap
        i1 = in1.ap if in1_ap is None else in1_ap
        a = sc_(scalar)
        P.op("dve", lambda e: e.scalar_tensor_tensor(out=o, in0=i0, scalar=a, in1=i1, op0=op0, op1=op1),
             reads=[in0, in1] + vs_(scalar), writes=[out])

    def CP(eng, out, in_, in_ap=None, out_ap=None):
        o = out.ap if out_ap is None else out_ap
        i = in_.ap if in_ap is None else in_ap
        P.op(eng, lambda e: e.tensor_copy(out=o, in_=i), reads=[in_], writes=[out])

    def MEMSET(eng, out, val):
        o = out.ap
        P.op(eng, lambda e: e.memset(o, val), writes=[out])

    P.dma("act", "cst_c", c16.v().ap, c_d.rearrange("(k p) -> k p", p=128), writes=[c16.v()])
    P.dma("act", "cst", b48.v().ap, bada_d.rearrange("(k p) -> k p", p=128), writes=[b48.v()])
    P.dma("act", "cst", ng16.v().ap, ng_d.rearrange("(k p) -> k p", p=128), writes=[ng16.v()])
    P.dma("act", "cst", sink16.v().ap, sink_d.unsqueeze(0), writes=[sink16.v()])
    P.dma("act", "cst", rowtmp.v(2, p=(0, 1)).ap, sb_d.unsqueeze(0), writes=[rowtmp.v(2)])
    P.dma("act", "cst", Wnat.v().ap, sw_d.rearrange("g t s -> t g s"), writes=[Wnat.v()])
    P.dma("act", "cst", lng_bc.v().ap, lng_d.partition_broadcast(128), writes=[lng_bc.v()])
    t_cst = P.dma("act", "cst", lnb_bc.v().ap, lnb_d.partition_broadcast(128), writes=[lnb_bc.v()])
    for key in list(P.recs):
        P.recs[key] = [(r[0], r[1], t_cst if r[2][0] == "cst" else r[2], r[3], r[4]) for r in P.recs[key]]

    MEMSET("pool", identf.v(), 1.0)
    _o = identf.v().ap
    P.op("pool", lambda e, _o=_o: e.affine_select(out=_o, in_=_o, pattern=[[-1, 128]], compare_op=ALU.is_equal,
                                          fill=0.0, base=0, channel_multiplier=1),
         reads=[identf.v()], writes=[identf.v()])
    for src, dst, k in ((c16, cT, 16), (b48, badaT, 48), (ng16, ngT, 16)):
        TR(bank(7), src.v(), identf.v((0, k), p=(0, k)), signal=True, out_ap=bank(7, 0, k).ap)
        CP("dve", dst.v(), bank(7), in_ap=bank(7, 0, k).ap)
    P.dma("pool", "cst_m", masks.v().ap, mask_d.rearrange("p (a b) -> p a b", a=3), writes=[masks.v()])

    def w_src(dram, c0, ncols):
        return dram[:, c0:c0 + ncols].rearrange("(k p) n -> p k n", p=128)

    tiles = []
    for i in range(4):
        tiles += [("gs", i), ("u", i), ("ada", 16 + i)]
    tiles += [("vs", i) for i in range(4)]
    tiles += [("q", i) for i in range(4)] + [("kv", 0)]
    for i in range(4):
        tiles += [("ga", i), ("ada", 20 + i)]
    NT_RING = len(tiles)
    issued = [0]
    col0 = {"q": 0, "ga": 1280, "u": 2304, "vs": 3328, "gs": 4352}

    def issue_next():
        i = issued[0]
        if i >= NT_RING:
            return
        issued[0] += 1
        kind, j = tiles[i]
        slot = i % 4
        dst = ring[slot]
        sk = "ring%d" % slot
        if kind == "ada":
            P.dma("pool", sk, dst.v().ap, w_src(wada_d, j * 256, 256), writes=[dst.v()])
        elif kind == "kv":
            P.dma("pool", sk, dst.v().ap, w_src(win_d, 1024, 256), writes=[dst.v()])
        else:
            P.dma("pool", sk, dst.v().ap, w_src(win_d, col0[kind] + j * 256, 256), writes=[dst.v()])

    tile_pos = [0]

    def cur_tile(kind, j):
        i = tile_pos[0]
        assert tiles[i] == (kind, j), (tiles[i], kind, j)
        tile_pos[0] += 1
        return ring[i % 4]

    def issue_big(Tb):
        P.dma("pool", "abig%d" % (Tb % 4), abig[Tb % 4].v().ap, w_src(wada_d, Tb * 512, 512), writes=[abig[Tb % 4].v()])

    for k_ in range(4):
        Tb_ = [0, 4, 1, 5][k_]
        P.dma("pool", "abig%d" % k_, abig[k_].v().ap, w_src(wada_d, Tb_ * 512, 512), writes=[abig[k_].v()])

    MEMSET("pool", onesf.v(), 1.0)
    MEMSET("pool", ones_lh.v(), 0.0)
    MEMSET("pool", ones_lh.v(0, (0, 64)), 1.0)
    MEMSET("pool", ones_lh.v(1, (64, 128)), 1.0)
    MEMSET("pool", mhalf.v(), -0.5)
    MEMSET("pool", ssq.v(), 0.0)
    MEMSET("pool", ones1.v(), 1.0)
    MEMSET("pool", one11.v(), 1.0)
    CP("dve", ident.v(), identf.v())

    ACTF(tT.v(), cT.v(), AF.Tanh, scale=0.5)
    STT(cactT.v(), tT.v(), 1.0, cT.v(), ALU.add, ALU.mult)

    def hilo_row(src, hi, lo, hif):
        CP("dve", hi.v(), src)
        CP("dve", hif, hi.v())
        TT("dve", lo.v(), src, hif, ALU.subtract)

    ACTF(es16.v(), sink16.v(), AF.Exp)
    es_f = rowtmp.v(0, p=(0, 1))
    hif = rowtmp.v(1, p=(0, 1))
    CP("dve", es_f, es16.v(), in_ap=es16.v().ap.unsqueeze(2).broadcast_to([1, 16, 64]),
       out_ap=es_f.ap.rearrange("p (h d) -> p h d", d=64))
    hilo_row(es_f, es_hi, es_lo, hif)
    bs_f = rowtmp.v(2, p=(0, 1))
    TS("dve", bs_f, bs_f, 0.5, None, ALU.mult)
    hilo_row(bs_f, bs_hi, bs_lo, hif)

    for r in range(2):
        for gg in range(4):
            g = 4 * r + gg
            TR(bank(7), Wnat.v(g), identf.v(), signal=(gg == 3), out_ap=bank(7, gg * 128, gg * 128 + 128).ap)
        CP("dve", Wtmp.v((4 * r, 4 * r + 4)), bank(7), in_ap=bank3(7, 4).ap)
    _w = Wtmp.v().ap
    P.op("pool", lambda e, _w=_w: e.affine_select(out=_w, in_=_w, pattern=[[0, 8], [1, 128]], compare_op=ALU.is_ge,
                                          fill=0.0, base=0, channel_multiplier=-1),
         reads=[Wtmp.v()], writes=[Wtmp.v()])
    ACTF(WT.v(), Wtmp.v(), AF.Copy, scale=0.5)

    for mi_ in range(3):
        TS("dve", maskb.v(mi_), masks.v(mi_), -1.0, 30000.0, ALU.add, ALU.mult,
           in0_ap=masks.v(mi_).ap.unsqueeze(1).broadcast_to([128, 4, 128]))
    maybe_stop('setup')
    def rstd_ops(src, dst, col):
        cv = (col, col + 1)
        TS("pool", rtmp.v(cv), src.v(cv), 1.0 / D, EPS, ALU.mult, ALU.add)
        TT("pool", dst.v(cv), rtmp.v(cv), mhalf.v((0, 1)), ALU.pow)

    def x_front(b):
        s = b % 2
        P.dma("sp", "xs%d" % s, xs[s].v().ap, x_d[b * 128:(b + 1) * 128, :], writes=[xs[s].v()])
        ACTF(sq.v(), xs[s].v(), AF.Square, accum=ssq.v((b, b + 1)))
        rstd_ops(ssq, rstd, b)
        TS("dve", xn[s].v(), xs[s].v(), rstd.v((b, b + 1)), None, ALU.mult)

    def x_back(b):
        s = b % 2
        pb = pair_bf(s)
        for c in range(16):
            TR(pb, xn[s].v((c * 128, c * 128 + 128)), ident.v(), signal=(c == 15), out_ap=pb.ap[:, c, :])
        tk = (b * 128, b * 128 + 128)
        ACTF(hT.v((0, 8), tk), bank(2 * s), AF.Copy, in_ap=pb.ap[:, 0:8, :])
        CP("dve", hT.v((8, 16), tk), bank(2 * s + 1), in_ap=pb.ap[:, 8:16, :])

    pend_tr = []

    def ada_transposes(t):
        mr = modrow[t % 2]
        for j in range(2):
            col = 2 * t + j
            MM(bank(6), mr.v((j * 128, j * 128 + 128)), one11.v(), True, True, signal=(j == 1),
               out_ap=bank(6, col, col + 1).ap)

    def ada_tile(t):
        w = cur_tile("ada", t)
        pm = bank(4 + t % 2, 0, 256, p=(0, 1))
        for kc in range(16):
            MM(pm, cactT.v((kc, kc + 1)), w.v(kc), kc == 0, kc == 15, signal=(kc == 15))
        issue_next()
        ACTF(modrow[t % 2].v((0, 256)), pm, AF.Copy)
        if pend_tr:
            ada_transposes(pend_tr.pop())
        pend_tr.append(t)

    def big_transposes(Tb):
        mr = modrow[Tb % 2]
        for j in range(4):
            col = 4 * Tb + j
            MM(bank(6), mr.v((j * 128, j * 128 + 128)), one11.v(), True, True, signal=(j == 3),
               out_ap=bank(6, col, col + 1).ap)

    big_order = [0, 4, 1, 5, 2, 6, 3, 7]

    def issue_big_seq(k):
        Tb = big_order[k]
        P.dma("pool", "abig%d" % (k % 4), abig[k % 4].v().ap, w_src(wada_d, Tb * 512, 512), writes=[abig[k % 4].v()])

    def ada_big(k):
        Tb = big_order[k]
        w = abig[k % 4]
        pm = bank(4 + k % 2, 0, 512, p=(0, 1))
        for kc in range(16):
            MM(pm, cactT.v((kc, kc + 1)), w.v(kc), kc == 0, kc == 15, signal=(kc == 15))
        if k + 4 < 8:
            issue_big_seq(k + 4)
        elif k in (4, 5):
            issue_next()
            issue_next()
        ACTF(modrow[Tb % 2].v(), pm, AF.Copy)
        big_transposes(Tb)

    def modulate_group(i):
        cs, ss = (4 * i, 4 * i + 4), (16 + 4 * i, 20 + 4 * i)
        STT(modT.v(cs), bank(6), 0.5, badaT.v(cs), ALU.mult, ALU.add, in0_ap=bank(6, cs[0], cs[1]).ap)
        STT(modT.v(ss), bank(6), 0.5, badaT.v(ss), ALU.mult, ALU.add, in0_ap=bank(6, ss[0], ss[1]).ap)
        STT(geff.v(cs), modT.v(ss), 1.0, ngT.v(cs), ALU.add, ALU.mult)
        for c in range(cs[0], cs[1]):
            TS("dve", hT.v(c), hT.v(c), geff.v((c, c + 1)), modT.v((c, c + 1)), ALU.mult, ALU.add)

    x_front(0)
    x_front(1)
    pending_groups = []
    x_done = False
    for k in range(8):
        ada_big(k)
        for b_ in (2 * k, 2 * k + 1):
            if b_ < 9:
                x_back(b_)
        for b_ in (2 * k + 2, 2 * k + 3):
            if b_ < 9:
                x_front(b_)
        if 2 * k + 1 >= 8:
            x_done = True
        if k % 2 == 1:
            pending_groups.append(k // 2)
        if x_done:
            while pending_groups:
                modulate_group(pending_groups.pop(0))

    maybe_stop('A')
    fm_count = [0]

    def fm_tile(w, evac):
        for j in range(2):
            pi = fm_count[0] % 2
            fm_count[0] += 1
            for kc in range(16):
                for hf in range(2):
                    MM(bank(2 * pi + hf), w.v(kc, (j * 128, j * 128 + 128)),
                       hT.v(kc, (128 + hf * 512, 128 + hf * 512 + 512)), kc == 0, kc == 15,
                       signal=(kc == 15 and hf == 1))
            if j == 1:
                issue_next()
            evac(j, pi)

    def silu2_evac(dst, oc, pi, tnh):
        tb = tnh[pi]
        ACTF(tb.v(), pair(pi), AF.Tanh, scale=0.5)
        STT(dst.v(oc), tb.v(), 1.0, pair(pi), ALU.add, ALU.mult)

    for i in range(4):
        w = cur_tile("gs", i)
        fm_tile(w, lambda j, pi, i=i: silu2_evac(ugT, 2 * i + j, pi, tnhC))
        w = cur_tile("u", i)
        fm_tile(w, lambda j, pi, i=i: TT("dve", ugT.v(2 * i + j), pair(pi), ugT.v(2 * i + j), ALU.mult))
        ada_tile(16 + i)
    ada_transposes(pend_tr.pop())
    STT(modT.v((32, 40)), bank(6), 0.5, badaT.v((32, 40)), ALU.mult, ALU.add, in0_ap=bank(6, 32, 40).ap)

    c2cnt = [0]

    def c2_block(b):
        tok = (b * 128, b * 128 + 128)
        for half in range(2):
            bk = c2cnt[0] % 2
            c2cnt[0] += 1
            for gg in range(4):
                g = 4 * half + gg
                dst_ap = bank(bk, gg * 128, gg * 128 + 128).ap
                MM(bank(bk), vn.v(b, (g * 128, g * 128 + 128)), WT.v(g), True, False, signal=False, out_ap=dst_ap)
                for k_, bs in enumerate((bs_hi, bs_lo)):
                    MM(bank(bk), ones1.v(), bs.v((g * 128, g * 128 + 128)), False, k_ == 1,
                       signal=(k_ == 1 and gg == 3), out_ap=dst_ap)
            TT("dve", aT.v((8 + 4 * half, 12 + 4 * half), tok), bank(bk), ugT.v((4 * half, 4 * half + 4), tok),
               ALU.mult, in0_ap=bank3(bk, 4).ap)

    wv = [cur_tile("vs", i) for i in range(4)]

    def ln_finish(b):
        pi = 1 + b % 3
        lt_ = lnt[b % 2]
        TS("dve", lt_.v(), pair(pi), lnst.v(b, (12, 13)), lnst.v(b, (15, 16)), ALU.subtract, ALU.mult)
        TT("dve", lt_.v(), lt_.v(), lng_bc.v(), ALU.mult)
        TT("dve", vn.v(b), lt_.v(), lnb_bc.v(), ALU.add)

    for b in range(NB):
        pi = 1 + b % 3
        tok = (128 + b * 128, 128 + b * 128 + 128)
        for vt in range(4):
            for kc in range(16):
                MM(pair(pi), hT.v(kc, tok), wv[vt].v(kc), kc == 0, kc == 15, signal=(kc == 15 and vt == 3),
                   out_ap=pp[pi][:, vt * 256:vt * 256 + 256])
        if b == NB - 1:
            for _ in range(4):
                issue_next()
        for hh in range(2):
            _o = lnst.v(b, (6 * hh, 6 * hh + 6))
            _i = bank(2 * pi + hh)
            P.op("dve", lambda e, o=_o.ap, i=_i.ap: e.bn_stats(out=o, in_=i), reads=[_i], writes=[_o])
        _o = lnst.v(b, (12, 14))
        _i = lnst.v(b, (0, 12))
        P.op("dve", lambda e, o=_o.ap, i=_i.ap: e.bn_aggr(out=o, in_=i), reads=[_i], writes=[_o])
        TS("pool", lnst.v(b, (14, 15)), lnst.v(b, (13, 14)), EPS, None, ALU.add)
        TT("pool", lnst.v(b, (15, 16)), lnst.v(b, (14, 15)), mhalf.v((0, 1)), ALU.pow)
        if b >= 1:
            ln_finish(b - 1)
    ln_finish(NB - 1)

    def issue_wo(bt):
        P.dma("pool", "wo%d" % bt, wo_big[bt].v().ap, w_src(wout_d, bt * 512, 512), writes=[wo_big[bt].v()])


    maybe_stop('C1')
    for b in range(NB):
        c2_block(b)

    maybe_stop('C2')
    for i in range(4):
        w = cur_tile("q", i)

        def evac_q(j, pi, i=i):
            oc = 2 * i + j
            if oc % 2 == 0:
                ACTF(qT.v(oc), pair(pi), AF.Copy, scale=0.125)
            else:
                TS("dve", qT.v(oc), pair(pi), 0.125, None, ALU.mult)
        fm_tile(w, evac_q)

    maybe_stop('B1q')
    w = cur_tile("kv", 0)
    pieces = [(0, 512), (512, 1024), (1024, TH)]
    for kc in range(16):
        for pi_, (a, b_) in enumerate(pieces):
            MM(bank(pi_), w.v(kc, (0, 128)), hT.v(kc, (a, b_)), kc == 0, kc == 15,
               signal=(kc == 15 and pi_ == 2), out_ap=bank(pi_, 0, b_ - a).ap)
    for pi_, (a, b_) in enumerate(pieces):
        for g in range(2):
            src_ap = bank(pi_, 0, b_ - a, p=(g * 64, g * 64 + 64)).ap
            for half in range(2):
                dstv = kdup.v(g, (a, b_), p=(half * 64, half * 64 + 64))
                if pi_ == 0 or (pi_ == 2 and g == 0):
                    ACTF(dstv, bank(pi_), AF.Copy, in_ap=src_ap)
                else:
                    CP("dve", dstv, bank(pi_), in_ap=src_ap)

    maybe_stop('B1k')
    MEMSET("pool", Vpad.v(), 0.0)
    for b in range(9):
        pv = bank(4 + b % 2)
        pv_ap = bank(4 + b % 2, 0, 128).ap
        for kc in range(16):
            MM(pv, hT.v(kc, (b * 128, b * 128 + 128)), w.v(kc, (128, 256)), kc == 0, kc == 15, signal=(kc == 15),
               out_ap=pv_ap)
        pv3 = pv_ap.rearrange("p (g d) -> p g d", g=2)
        TS("dve", Vpad.v(b, None, 0, (0, 64)), pv, 0.5, None, ALU.mult, in0_ap=pv3)
        ACTF(Vpad.v(b, None, 1, (64, 128)), pv, AF.Copy, scale=0.5, in_ap=pv3)
    issue_next()

    maybe_stop('B1v')
    for i in range(4):
        w = cur_tile("ga", i)
        fm_tile(w, lambda j, pi, i=i: silu2_evac(gateT, 2 * i + j, pi, tnhB))
        ada_tile(20 + i)
    ada_transposes(pend_tr.pop())
    STT(modT.v((40, 48)), bank(6), 0.5, badaT.v((40, 48)), ALU.mult, ALU.add, in0_ap=bank(6, 40, 48).ap)

    maybe_stop('B1')
    for bt in (2, 3, 0, 1):
        issue_wo(bt)
    def att_stage1(unit):
        n, g = unit // 2, unit % 2
        u2 = unit % 2
        qtok = (n * 128, n * 128 + 128)
        for kcx in range(2):
            ktok = ((n + kcx) * 128, (n + kcx) * 128 + 128)
            sp_ = pair(kcx)
            mi = 2 if kcx == 1 else (0 if n == 0 else 1)
            for h in range(8):
                c = 4 * g + h // 2
                rows = ((h % 2) * 64, (h % 2) * 64 + 64)
                sl = (h % 2) * 4 + h // 2
                MM(sp_, kdup.v(g, ktok, p=rows), qT.v(c, qtok, p=rows), True, True, signal=(h == 7),
                   out_ap=pp[kcx][:, sl * 128:sl * 128 + 128])
            pt = PT[2 * u2 + kcx]
            ACTF(pt.v(), sp_, AF.Exp, in_ap=sp_.ap.rearrange("p (h q) -> p h q", h=8))
            TT("dve", pt.v(), pt.v(), masks.v(mi), ALU.mult,
               in1_ap=masks.v(mi).ap.unsqueeze(1).broadcast_to([128, 8, 128]))

    def att_stage2(unit):
        n, g = unit // 2, unit % 2
        u2 = unit % 2
        qtok = (n * 128, n * 128 + 128)
        pts = [PT[2 * u2], PT[2 * u2 + 1]]
        def slots(kcx, par):
            v_ = pts[kcx].v((par * 4, par * 4 + 4))
            return v_, v_.ap.rearrange("p a b -> p (a b)")

        denb_ = bank(6 + u2)
        first = True
        for kcx in range(2):
            for par in range(2):
                v_, flat = slots(kcx, par)
                MM(denb_, ones_lh.v(par), v_, first, False, signal=False, rhs_ap=flat)
                first = False
        for pc in range(4):
            c = 4 * g + pc
            dst_ap = bank(6 + u2, pc * 128, pc * 128 + 128).ap
            for k_, es in enumerate((es_hi, es_lo)):
                lastd = (pc == 3 and k_ == 1)
                MM(denb_, es.v((c * 128, c * 128 + 128)), ones1.v(), False, lastd, signal=lastd, out_ap=dst_ap)
        pob_ = bank(4 + u2)
        first = True
        for kcx in range(2):
            for par in range(2):
                v_, flat = slots(kcx, par)
                lastp = (kcx == 1 and par == 1)
                MM(pob_, Vpad.v(n + kcx, g, par), v_, first, lastp, signal=lastp, rhs_ap=flat)
                first = False
        CP_den = bank3(6 + u2, 4)
        _r = rden[u2].v().ap
        _d = CP_den.ap
        ACTF(rden[u2].v(), CP_den, AF.Ln)
        ACTF(rden[u2].v(), rden[u2].v(), AF.Exp, scale=-1.0)
        TT("dve", g2[u2].v(), gateT.v((4 * g, 4 * g + 4), qtok), rden[u2].v(), ALU.mult)
        TT("dve", aT.v((4 * g, 4 * g + 4), qtok), bank(4 + u2), g2[u2].v(), ALU.mult, in0_ap=bank3(4 + u2, 4).ap)

    NU = 2 * NB
    for unit in range(NU + 1):
        if unit < NU:
            att_stage1(unit)
        if unit >= 1:
            att_stage2(unit - 1)

    maybe_stop('B2')
    P.dma("sp", "cst_fg", fg_bc.v().ap, fg_d.partition_broadcast(128), writes=[fg_bc.v()])
    for c in range(16):
        ACTF(Rg.v(c), identf.v(), AF.Copy, scale=modT.v((32 + c, 33 + c)))
    for j in range(4):
        bk = j % 2
        MM(bank(bk), onesf.v(), Rg.v((4 * j, 4 * j + 4)), True, True, signal=True,
           rhs_ap=Rg.v((4 * j, 4 * j + 4)).ap.rearrange("p a b -> p (a b)"))
        ACTF(gate_bc.v((j * 512, j * 512 + 512)), bank(bk), AF.Copy)

    maybe_stop('G')
    def d_finish(b):
        s = b % 2
        col = 16 + b
        ACTF(od[s].v(), xd[s].v(), AF.Square, accum=ssq.v((col, col + 1)))
        rstd_ops(ssq, rstd, col)
        STT(od[s].v(), xd[s].v(), rstd.v((col, col + 1)), fg_bc.v(), ALU.mult, ALU.mult)
        P.dma("sp", "od%d" % s, y_d[b * 128:(b + 1) * 128, :], od[s].v().ap, reads=[od[s].v()], final=True)

    for b in range(NB):
        s = b % 2
        tok = (b * 128, b * 128 + 128)
        P.dma("sp", "xd%d" % s, xd[s].v().ap, x_d[128 + b * 128:128 + (b + 1) * 128, :], writes=[xd[s].v()])
        for kc in range(16):
            for bt in range(4):
                MM(bank(4 * s + bt), aT.v(kc, tok), wo_big[bt].v(kc), kc == 0, kc == 15, signal=(kc == 15))
        for bt in range(4):
            cs = (bt * 512, bt * 512 + 512)
            tm = tmpD[bt % 2]
            TT("dve", tm.v(), bank(4 * s + bt), gate_bc.v(cs), ALU.mult)
            TT("dve", xd[s].v(cs), xd[s].v(cs), tm.v(), ALU.add)
        if b >= 1:
            d_finish(b - 1)
    d_finish(NB - 1)

    assert tile_pos[0] == NT_RING and issued[0] == NT_RING, (tile_pos[0], issued[0], NT_RING)


_MASKS = None


def _masks(has_prev):
    j = np.arange(128)[:, None]
    i = np.arange(128)[None, :]
    mp = (j > i).astype(np.float32)
    mc = (j <= i).astype(np.float32)
    m0 = mp * (1.0 if has_prev else 0.0)
    return np.ascontiguousarray(np.stack([m0, mp, mc], axis=1).reshape(128, 3 * 128))


def kernel(x, c, norm_g, w_ada, b_ada, w_in, attn_sinks, sgu_ln_g, sgu_ln_b, sgu_w, sgu_b, w_out, final_g):
    x = np.asarray(x, np.float32)
    B, S, _ = x.shape
    shared = {
        "norm_g": np.ascontiguousarray(np.asarray(norm_g, np.float32)[0]),
        "w_ada": np.ascontiguousarray(np.asarray(w_ada, np.float32)[0]),
        "b_ada": np.ascontiguousarray(np.asarray(b_ada, np.float32)[0]),
        "w_in": np.ascontiguousarray(np.asarray(w_in, np.float32)[0]),
        "sinks": np.ascontiguousarray(np.asarray(attn_sinks, np.float32)[0]),
        "ln_g": np.ascontiguousarray(np.asarray(sgu_ln_g, np.float32)[0]),
        "ln_b": np.ascontiguousarray(np.asarray(sgu_ln_b, np.float32)[0]),
        "sgu_w": np.ascontiguousarray(np.asarray(sgu_w, np.float32)[0]),
        "sgu_b": np.ascontiguousarray(np.asarray(sgu_b, np.float32)[0].reshape(-1)),
        "w_out": np.ascontiguousarray(np.asarray(w_out, np.float32)[0]),
        "final_g": np.ascontiguousarray(np.asarray(final_g, np.float32)),
    }
    c = np.asarray(c, np.float32)
    in_maps = []
    for j in range(NCORES):
        b, half = j // 2, j % 2
        xs = np.zeros((TH, D), np.float32)
        xs[128:] = x[b, half * T:(half + 1) * T]
        if half == 1:
            xs[:128] = x[b, T - 128:T]
        m = dict(shared)
        m["x"] = xs
        m["c"] = np.ascontiguousarray(c[b])
        m["masks"] = _masks(half == 1)
        in_maps.append(m)
    nc = build_nc()
    res = run_bass_kernel_spmd(nc, in_maps, core_ids=list(range(NCORES)))
    out = np.empty((B, S, D), np.float32)
    for j in range(NCORES):
        b, half = j // 2, j % 2
        out[b, half * T:(half + 1) * T] = res.results[j]["y"]
    return out
```

```python
import numpy as np
import concourse.bass as bass
import concourse.mybir as mybir
from concourse.bass_utils import run_bass_kernel_spmd

F32 = mybir.dt.float32
BF16 = mybir.dt.bfloat16
ALU = mybir.AluOpType
AF = mybir.ActivationFunctionType

D = 2048
KC = 16
T = 1024
TH = T + 128
NB = 8
D_IN = 5376
EPS = 1e-6
NCORES = 8


class V:
    __slots__ = ("ap", "rng")

    def __init__(self, ap, rng):
        self.ap = ap
        self.rng = rng


class LT:
    def __init__(self, key, base_ap, byte_off, dims, dtype):
        self.key = key
        self.off = byte_off
        self.dims = list(dims)
        self.ds = 2 if dtype == BF16 else 4
        if len(dims) == 1:
            self.ap = base_ap
        elif len(dims) == 2:
            self.ap = base_ap.rearrange("p (a b) -> p a b", a=dims[0])
        elif len(dims) == 3:
            self.ap = base_ap.rearrange("p (a b c) -> p a b c", a=dims[0], b=dims[1])
        elif len(dims) == 4:
            self.ap = base_ap.rearrange("p (a b c d) -> p a b c d", a=dims[0], b=dims[1], c=dims[2])
        else:
            raise ValueError

    def v(self, *idx, p=None):
        idx = list(idx) + [None] * (len(self.dims) - len(idx))
        key = [slice(None) if p is None else slice(p[0], p[1])]
        norm = []
        for i, d in zip(idx, self.dims):
            if i is None:
                key.append(slice(None)); norm.append((0, d))
            elif isinstance(i, int):
                key.append(i); norm.append((i, i + 1))
            else:
                key.append(slice(i[0], i[1])); norm.append((i[0], i[1]))
        ap = self.ap[tuple(key)]
        n = len(self.dims)
        strides = [1] * n
        for k in range(n - 2, -1, -1):
            strides[k] = strides[k + 1] * self.dims[k + 1]
        k = n - 1
        while k > 0 and norm[k] == (0, self.dims[k]):
            k -= 1
        outer = [range(a, b) for (a, b) in norm[:k]]
        cnt = 1
        for r in outer:
            cnt *= len(r)
        rng = []
        if cnt > 48:
            lo = sum(norm[j][0] * strides[j] for j in range(n))
            hi = sum((norm[j][1] - 1) * strides[j] for j in range(n)) + 1
            rng.append((self.key, self.off + lo * self.ds, self.off + hi * self.ds))
        else:
            import itertools
            for combo in itertools.product(*outer):
                base = sum(c * strides[j] for j, c in enumerate(combo))
                lo = base + norm[k][0] * strides[k]
                hi = base + norm[k][1] * strides[k]
                rng.append((self.key, self.off + lo * self.ds, self.off + hi * self.ds))
        return V(ap, rng)


class Prog:
    ENG = ("pe", "act", "dve", "pool", "sp")

    def __init__(self, nc):
        self.nc = nc
        self.q = {e: [] for e in self.ENG}
        self.sems = {}
        self.cnt = {}
        for e in ("pe", "act", "dve", "pool"):
            self.sems[e] = nc.alloc_semaphore("s_" + e)
            self.cnt[e] = 0
        self.seen = {e: {} for e in self.ENG}
        self.recs = {}
        self.pe_pending = False
        self.final_waits = []

    def _deps(self, reads, writes, eng=None):
        waits = []
        for v in reads:
            for (key, lo, hi) in v.rng:
                for r in self.recs.get(key, ()):
                    if (r[3] or (key == "psum" and r[4] != eng)) and r[0] < hi and lo < r[1]:
                        waits.append(r[2])
        for v in writes:
            for (key, lo, hi) in v.rng:
                for r in self.recs.get(key, ()):
                    if r[0] < hi and lo < r[1]:
                        waits.append(r[2])
        return waits

    def _record(self, eng, ticket, reads, writes):
        for v in writes:
            for (key, lo, hi) in v.rng:
                lst = self.recs.setdefault(key, [])
                lst[:] = [r for r in lst if not (lo <= r[0] and r[1] <= hi)]
                lst.append((lo, hi, ticket, True, eng))
        for v in reads:
            for (key, lo, hi) in v.rng:
                lst = self.recs.setdefault(key, [])
                for i, r in enumerate(lst):
                    if (not r[3]) and r[4] == eng and r[0] == lo and r[1] == hi:
                        lst[i] = (lo, hi, ticket, False, eng)
                        break
                else:
                    lst.append((lo, hi, ticket, False, eng))

    def _emit_waits(self, eng, waits):
        best = {}
        for (sk, val, src) in waits:
            if src == "pe" and eng == "pe":
                continue
            if val > best.get(sk, 0):
                best[sk] = val
        for sk, val in best.items():
            if self.seen[eng].get(sk, 0) >= val:
                continue
            self.seen[eng][sk] = val
            self.q[eng].append(("w", sk, val))

    def op(self, eng, fn, reads=(), writes=(), signal=True):
        self._emit_waits(eng, self._deps(reads, writes, eng))
        if signal:
            self.cnt[eng] += 1
            ticket = (eng, self.cnt[eng], eng)
            self.q[eng].append(("i", fn, eng, 1))
            if eng == "pe":
                self.pe_pending = False
        else:
            assert eng == "pe"
            ticket = (eng, self.cnt[eng] + 1, eng)
            self.q[eng].append(("i", fn, None, 0))
            self.pe_pending = True
        self._record(eng, ticket, reads, writes)
        return ticket

    def dma(self, eng, semkey, out_ap, in_ap, reads=(), writes=(), final=False, noncontig=False):
        if semkey not in self.sems:
            self.sems[semkey] = self.nc.alloc_semaphore("d_" + semkey)
            self.cnt[semkey] = 0
        self._emit_waits(eng, self._deps(reads, writes))
        self.cnt[semkey] += 16
        ticket = (semkey, self.cnt[semkey], "dma")
        if noncontig:
            def fn(e, o=out_ap, i=in_ap):
                with self.nc.allow_non_contiguous_dma(reason="small strided param load"):
                    return e.dma_start(out=o, in_=i)
        else:
            def fn(e, o=out_ap, i=in_ap):
                return e.dma_start(out=o, in_=i)
        self.q[eng].append(("i", fn, semkey, 16))
        self._record("dma", ticket, reads, writes)
        if final:
            self.final_waits.append(ticket)
        return ticket

    def finish(self):
        assert not self.pe_pending
        self._emit_waits("sp", self.final_waits)
        nc = self.nc
        sems = self.sems
        q = self.q

        def run(e, items):
            for it in items:
                if it[0] == "w":
                    e.wait_ge(sems[it[1]], it[2])
                else:
                    ins = it[1](e)
                    if it[2] is not None:
                        ins.then_inc(sems[it[2]], it[3])

        with nc.Block() as block:
            @block.sync
            def _(e):
                run(e, q["sp"])

            @block.gpsimd
            def _(e):
                run(e, q["pool"])

            @block.scalar
            def _(e):
                run(e, q["act"])

            @block.vector
            def _(e):
                run(e, q["dve"])

            @block.tensor
            def _(e):
                run(e, q["pe"])


class _Stop(Exception):
    pass


def build_nc(stop=None):
    nc = bass.Bass("TRN2", target_bir_lowering=False)
    P = Prog(nc)
    try:
        _build_body(nc, P, stop)
    except _Stop:
        pass
    P.finish()
    return nc


def _build_body(nc, P, stop):
    def maybe_stop(tag):
        if stop == tag:
            raise _Stop()

    def din(name, shape):
        return nc.dram_tensor(name, list(shape), F32, kind="ExternalInput").ap()

    x_d = din("x", [TH, D])
    c_d = din("c", [D])
    ng_d = din("norm_g", [D])
    wada_d = din("w_ada", [D, 3 * D])
    bada_d = din("b_ada", [3 * D])
    win_d = din("w_in", [D, D_IN])
    sink_d = din("sinks", [16])
    lng_d = din("ln_g", [1024])
    lnb_d = din("ln_b", [1024])
    sw_d = din("sgu_w", [8, 128, 128])
    sb_d = din("sgu_b", [1024])
    wout_d = din("w_out", [D, D])
    fg_d = din("final_g", [D])
    mask_d = din("masks", [128, 3 * 128])
    y_d = nc.dram_tensor("y", [T, D], F32, kind="ExternalOutput").ap()

    def arena(name, nbytes):
        return nc.alloc_sbuf_tensor(name, [128, nbytes // 4], F32)

    def lt_in(ar, key, off, dims, dtype):
        n = 1
        for d_ in dims:
            n *= d_
        ds = 2 if dtype == BF16 else 4
        base = ar[:, off // 4:(off + n * ds) // 4]
        if dtype == BF16:
            base = base.bitcast(BF16)
        return LT(key, base, off, dims, dtype)

    def own(name, dims, dtype, parts=128):
        n = 1
        for d_ in dims:
            n *= d_
        h = nc.alloc_sbuf_tensor("sb_" + name, [parts, n], dtype)
        return LT(name, h[:, :], 0, dims, dtype)

    aT_h = nc.alloc_sbuf_tensor("sb_aT", [128, 16 * T], BF16)
    aT = LT("aT", aT_h[:, :], 0, [16, T], BF16)
    aT_f32 = aT_h[:, :].bitcast(F32)

    def aT_tmp(off, dims):
        n = 1
        for d_ in dims:
            n *= d_
        return LT("aT", aT_f32[:, off // 4:(off + 4 * n) // 4], off, dims, F32)
    ring_ar = arena("ring", 32768)
    ring = [lt_in(ring_ar, "ring", s * 8192, [16, 256], BF16) for s in range(4)]
    wo_big = [lt_in(ring_ar, "ring", s * 16384, [16, 512], BF16) for s in range(2)]
    hT_ar = arena("hTa", 36864)
    hT = lt_in(hT_ar, "hTa", 0, [16, TH], BF16)
    wo_big += [lt_in(hT_ar, "hTa", s * 16384, [16, 512], BF16) for s in range(2)]
    S = arena("S", 71680)

    def s_lt(off, dims, dtype):
        return lt_in(S, "S", off, dims, dtype)

    xs = [s_lt(0, [D], F32), s_lt(8192, [D], F32)]
    xn = [s_lt(16384, [D], BF16), s_lt(20480, [D], BF16)]
    sq = s_lt(24576, [D], BF16)
    Wnat = aT_tmp(0, [8, 128])
    Wtmp = aT_tmp(4096, [8, 128])
    rowtmp = aT_tmp(8192, [3, 1024])
    abig = [lt_in(ring_ar, "ring", 0, [16, 512], BF16), lt_in(ring_ar, "ring", 16384, [16, 512], BF16),
            s_lt(28672, [16, 512], BF16), s_lt(45056, [16, 512], BF16)]
    qT = s_lt(0, [8, T], BF16)
    gateT = s_lt(16384, [8, T], BF16)
    kdup = s_lt(32768, [2, TH], BF16)
    Vpad = s_lt(37376, [9, 2, 2, 128], BF16)
    PT = [s_lt(46592 + i * 2048, [8, 128], BF16) for i in range(4)]
    rden = [s_lt(54784 + i * 2048, [4, 128], F32) for i in range(2)]
    g2 = [s_lt(58880 + i * 2048, [4, 128], F32) for i in range(2)]
    tnhB = [s_lt(62976 + i * 4096, [T], F32) for i in range(2)]
    ugT = s_lt(0, [8, T], BF16)
    vn = s_lt(16384, [8, 1024], BF16)
    lnt = [s_lt(32768 + i * 4096, [1024], F32) for i in range(2)]
    tnhC = [s_lt(40960 + i * 4096, [T], F32) for i in range(2)]
    xd = [s_lt(0, [D], F32), s_lt(8192, [D], F32)]
    od = [s_lt(16384, [D], F32), s_lt(24576, [D], F32)]
    tmpD = [s_lt(32768 + i * 2048, [512], F32) for i in range(2)]
    Rg = s_lt(40960, [16, 128], F32)
    gate_bc = s_lt(49152, [D], F32)
    fg_bc = s_lt(57344, [D], F32)

    ident = own("ident", [128], BF16)
    identf = own("identf", [128], F32)
    onesf = own("onesf", [128], F32)
    masks = own("masks", [3, 128], BF16)
    WT = own("WT", [8, 128], BF16)
    maskb = own("maskb", [3, 4, 128], BF16)
    ones_lh = own("ones_lh", [2, 128], BF16)
    lng_bc = own("lng_bc", [1024], F32)
    lnb_bc = own("lnb_bc", [1024], F32)
    cT = own("cT", [16], F32)
    c16 = own("c16", [128], F32, parts=16)
    b48 = own("b48", [128], F32, parts=48)
    ng16 = own("ng16", [128], F32, parts=16)
    tT = own("tT", [16], F32)
    cactT = own("cactT", [16], BF16)
    badaT = own("badaT", [48], F32)
    ngT = own("ngT", [16], F32)
    modT = own("modT", [48], F32)
    geff = own("geff", [16], F32)
    ssq = own("ssq", [32], F32)
    rtmp = own("rtmp", [32], F32)
    rstd = own("rstd", [32], F32)
    mhalf = own("mhalf", [32], F32)
    lnst = own("lnst", [8, 16], F32)
    sink16 = own("sink16", [16], F32, parts=1)
    es16 = own("es16", [16], F32, parts=1)
    es_hi = own("es_hi", [1024], BF16, parts=1)
    es_lo = own("es_lo", [1024], BF16, parts=1)
    bs_hi = own("bs_hi", [1024], BF16, parts=1)
    bs_lo = own("bs_lo", [1024], BF16, parts=1)
    modrow = [own("modrow0", [512], F32, parts=1), own("modrow1", [512], F32, parts=1)]
    ones1 = own("ones1", [128], BF16, parts=1)
    one11 = own("one11", [1], F32, parts=1)

    pp = [nc.alloc_psum_tensor("pp%d" % i, [128, 1024], F32) for i in range(4)]

    def bank(i, a=0, b=512, p=None):
        ps = slice(None) if p is None else slice(p[0], p[1])
        return V(pp[i // 2][ps, (i % 2) * 512 + a:(i % 2) * 512 + b], [("psum", i, i + 1)])

    def bank3(i, n):
        return V(pp[i // 2][:, (i % 2) * 512:(i % 2 + 1) * 512].rearrange("p (a b) -> p a b", a=n),
                 [("psum", i, i + 1)])

    def pair(i):
        return V(pp[i][:, :], [("psum", 2 * i, 2 * i + 2)])

    def pair_bf(i):
        return V(pp[i][:, :].bitcast(BF16).rearrange("p (c t) -> p c t", t=128),
                 [("psum", 2 * i, 2 * i + 2)])


    def vs_(x):
        return [x] if isinstance(x, V) else []

    def sc_(x):
        return x.ap if isinstance(x, V) else x

    def MM(out, lhs, rhs, start, stop, signal, rhs_ap=None, out_ap=None):
        o = out.ap if out_ap is None else out_ap
        l = lhs.ap
        r = rhs.ap if rhs_ap is None else rhs_ap
        P.op("pe", lambda e: e.matmul(o, lhsT=l, rhs=r, start=start, stop=stop),
             reads=[lhs, rhs], writes=[out], signal=signal)

    def TR(out, in_, idv, signal, out_ap=None):
        o = out.ap if out_ap is None else out_ap
        i = in_.ap
        d = idv.ap
        P.op("pe", lambda e: e.transpose(out=o, in_=i, identity=d), reads=[in_, idv], writes=[out], signal=signal)

    def ACTF(out, in_, func, scale=1.0, accum=None, in_ap=None, out_ap=None, extra_w=()):
        o = out.ap if out_ap is None else out_ap
        i = in_.ap if in_ap is None else in_ap
        kw = {}
        if accum is not None:
            kw["accum_out"] = accum.ap
        sc = sc_(scale)
        P.op("act", lambda e: e.activation(out=o, in_=i, func=func, scale=sc, **kw),
             reads=[in_] + vs_(scale), writes=[out] + vs_(accum) + list(extra_w))

    def TS(eng, out, in0, s1, s2, op0, op1=None, in0_ap=None, out_ap=None):
        o = out.ap if out_ap is None else out_ap
        i = in0.ap if in0_ap is None else in0_ap
        a1, a2 = sc_(s1), sc_(s2)
        kw = {} if op1 is None else {"op1": op1}
        P.op(eng, lambda e: e.tensor_scalar(out=o, in0=i, scalar1=a1, scalar2=a2, op0=op0, **kw),
             reads=[in0] + vs_(s1) + vs_(s2), writes=[out])

    def TT(eng, out, in0, in1, op, in0_ap=None, in1_ap=None, out_ap=None):
        o = out.ap if out_ap is None else out_ap
        i0 = in0.ap if in0_ap is None else in0_ap
        i1 = in1.ap if in1_ap is None else in1_ap
        P.op(eng, lambda e: e.tensor_tensor(out=o, in0=i0, in1=i1, op=op), reads=[in0, in1], writes=[out])

    def STT(out, in0, scalar, in1, op0, op1, in0_ap=None, in1_ap=None):
        o = out.ap
        i0 = in0.ap if in0_ap is None else in0_ap
        i1 = in1.ap if in1_ap is None else in1_ap
        a = sc_(scalar)
        P.op("dve", lambda e: e.scalar_tensor_tensor(out=o, in0=i0, scalar=a, in1=i1, op0=op0, op1=op1),
             reads=[in0, in1] + vs_(scalar), writes=[out])

    def CP(eng, out, in_, in_ap=None, out_ap=None):
        o = out.ap if out_ap is None else out_ap
        i = in_.ap if in_ap is None else in_ap
        P.op(eng, lambda e: e.tensor_copy(out=o, in_=i), reads=[in_], writes=[out])

    def MEMSET(eng, out, val):
        o = out.ap
        P.op(eng, lambda e: e.memset(o, val), writes=[out])

    P.dma("act", "cst_c", c16.v().ap, c_d.rearrange("(k p) -> k p", p=128), writes=[c16.v()])
    P.dma("act", "cst", b48.v().ap, bada_d.rearrange("(k p) -> k p", p=128), writes=[b48.v()])
    P.dma("act", "cst", ng16.v().ap, ng_d.rearrange("(k p) -> k p", p=128), writes=[ng16.v()])
    P.dma("act", "cst", sink16.v().ap, sink_d.unsqueeze(0), writes=[sink16.v()])
    P.dma("act", "cst", rowtmp.v(2, p=(0, 1)).ap, sb_d.unsqueeze(0), writes=[rowtmp.v(2)])
    P.dma("act", "cst", Wnat.v().ap, sw_d.rearrange("g t s -> t g s"), writes=[Wnat.v()])
    P.dma("act", "cst", lng_bc.v().ap, lng_d.partition_broadcast(128), writes=[lng_bc.v()])
    t_cst = P.dma("act", "cst", lnb_bc.v().ap, lnb_d.partition_broadcast(128), writes=[lnb_bc.v()])
    for key in list(P.recs):
        P.recs[key] = [(r[0], r[1], t_cst if r[2][0] == "cst" else r[2], r[3], r[4]) for r in P.recs[key]]

    MEMSET("pool", identf.v(), 1.0)
    _o = identf.v().ap
    P.op("pool", lambda e, _o=_o: e.affine_select(out=_o, in_=_o, pattern=[[-1, 128]], compare_op=ALU.is_equal,
                                          fill=0.0, base=0, channel_multiplier=1),
         reads=[identf.v()], writes=[identf.v()])
    for src, dst, k in ((c16, cT, 16), (b48, badaT, 48), (ng16, ngT, 16)):
        TR(bank(7), src.v(), identf.v((0, k), p=(0, k)), signal=True, out_ap=bank(7, 0, k).ap)
        CP("dve", dst.v(), bank(7), in_ap=bank(7, 0, k).ap)
    P.dma("pool", "cst_m", masks.v().ap, mask_d.rearrange("p (a b) -> p a b", a=3), writes=[masks.v()])

    def w_src(dram, c0, ncols):
        return dram[:, c0:c0 + ncols].rearrange("(k p) n -> p k n", p=128)

    tiles = []
    for i in range(4):
        tiles += [("gs", i), ("u", i), ("ada", 16 + i)]
    tiles += [("vs", i) for i in range(4)]
    tiles += [("q", i) for i in range(4)] + [("kv", 0)]
    for i in range(4):
        tiles += [("ga", i), ("ada", 20 + i)]
    NT_RING = len(tiles)
    issued = [0]
    col0 = {"q": 0, "ga": 1280, "u": 2304, "vs": 3328, "gs": 4352}

    def issue_next():
        i = issued[0]
        if i >= NT_RING:
            return
        issued[0] += 1
        kind, j = tiles[i]
        slot = i % 4
        dst = ring[slot]
        sk = "ring%d" % slot
        if kind == "ada":
            P.dma("pool", sk, dst.v().ap, w_src(wada_d, j * 256, 256), writes=[dst.v()])
        elif kind == "kv":
            P.dma("pool", sk, dst.v().ap, w_src(win_d, 1024, 256), writes=[dst.v()])
        else:
            P.dma("pool", sk, dst.v().ap, w_src(win_d, col0[kind] + j * 256, 256), writes=[dst.v()])

    tile_pos = [0]

    def cur_tile(kind, j):
        i = tile_pos[0]
        assert tiles[i] == (kind, j), (tiles[i], kind, j)
        tile_pos[0] += 1
        return ring[i % 4]

    def issue_big(Tb):
        P.dma("pool", "abig%d" % (Tb % 4), abig[Tb % 4].v().ap, w_src(wada_d, Tb * 512, 512), writes=[abig[Tb % 4].v()])

    for k_ in range(4):
        Tb_ = [0, 4, 1, 5][k_]
        P.dma("pool", "abig%d" % k_, abig[k_].v().ap, w_src(wada_d, Tb_ * 512, 512), writes=[abig[k_].v()])

    MEMSET("pool", onesf.v(), 1.0)
    MEMSET("pool", ones_lh.v(), 0.0)
    MEMSET("pool", ones_lh.v(0, (0, 64)), 1.0)
    MEMSET("pool", ones_lh.v(1, (64, 128)), 1.0)
    MEMSET("pool", mhalf.v(), -0.5)
    MEMSET("pool", ssq.v(), 0.0)
    MEMSET("pool", ones1.v(), 1.0)
    MEMSET("pool", one11.v(), 1.0)
    CP("dve", ident.v(), identf.v())

    ACTF(tT.v(), cT.v(), AF.Tanh, scale=0.5)
    STT(cactT.v(), tT.v(), 1.0, cT.v(), ALU.add, ALU.mult)

    def hilo_row(src, hi, lo, hif):
        CP("dve", hi.v(), src)
        CP("dve", hif, hi.v())
        TT("dve", lo.v(), src, hif, ALU.subtract)

    ACTF(es16.v(), sink16.v(), AF.Exp)
    es_f = rowtmp.v(0, p=(0, 1))
    hif = rowtmp.v(1, p=(0, 1))
    CP("dve", es_f, es16.v(), in_ap=es16.v().ap.unsqueeze(2).broadcast_to([1, 16, 64]),
       out_ap=es_f.ap.rearrange("p (h d) -> p h d", d=64))
    hilo_row(es_f, es_hi, es_lo, hif)
    bs_f = rowtmp.v(2, p=(0, 1))
    TS("dve", bs_f, bs_f, 0.5, None, ALU.mult)
    hilo_row(bs_f, bs_hi, bs_lo, hif)

    for r in range(2):
        for gg in range(4):
            g = 4 * r + gg
            TR(bank(7), Wnat.v(g), identf.v(), signal=(gg == 3), out_ap=bank(7, gg * 128, gg * 128 + 128).ap)
        CP("dve", Wtmp.v((4 * r, 4 * r + 4)), bank(7), in_ap=bank3(7, 4).ap)
    _w = Wtmp.v().ap
    P.op("pool", lambda e, _w=_w: e.affine_select(out=_w, in_=_w, pattern=[[0, 8], [1, 128]], compare_op=ALU.is_ge,
                                          fill=0.0, base=0, channel_multiplier=-1),
         reads=[Wtmp.v()], writes=[Wtmp.v()])
    ACTF(WT.v(), Wtmp.v(), AF.Copy, scale=0.5)

    for mi_ in range(3):
        TS("dve", maskb.v(mi_), masks.v(mi_), -1.0, 30000.0, ALU.add, ALU.mult,
           in0_ap=masks.v(mi_).ap.unsqueeze(1).broadcast_to([128, 4, 128]))
    maybe_stop('setup')
    def rstd_ops(src, dst, col):
        cv = (col, col + 1)
        TS("pool", rtmp.v(cv), src.v(cv), 1.0 / D, EPS, ALU.mult, ALU.add)
        TT("pool", dst.v(cv), rtmp.v(cv), mhalf.v((0, 1)), ALU.pow)

    def x_front(b):
        s = b % 2
        P.dma("sp", "xs%d" % s, xs[s].v().ap, x_d[b * 128:(b + 1) * 128, :], writes=[xs[s].v()])
        ACTF(sq.v(), xs[s].v(), AF.Square, accum=ssq.v((b, b + 1)))
        rstd_ops(ssq, rstd, b)
        TS("dve", xn[s].v(), xs[s].v(), rstd.v((b, b + 1)), None, ALU.mult)

    def x_back(b):
        s = b % 2
        pb = pair_bf(s)
        for c in range(16):
            TR(pb, xn[s].v((c * 128, c * 128 + 128)), ident.v(), signal=(c == 15), out_ap=pb.ap[:, c, :])
        tk = (b * 128, b * 128 + 128)
        ACTF(hT.v((0, 8), tk), bank(2 * s), AF.Copy, in_ap=pb.ap[:, 0:8, :])
        CP("dve", hT.v((8, 16), tk), bank(2 * s + 1), in_ap=pb.ap[:, 8:16, :])

    pend_tr = []

    def ada_transposes(t):
        mr = modrow[t % 2]
        for j in range(2):
            col = 2 * t + j
            MM(bank(6), mr.v((j * 128, j * 128 + 128)), one11.v(), True, True, signal=(j == 1),
               out_ap=bank(6, col, col + 1).ap)

    def ada_tile(t):
        w = cur_tile("ada", t)
        pm = bank(4 + t % 2, 0, 256, p=(0, 1))
        for kc in range(16):
            MM(pm, cactT.v((kc, kc + 1)), w.v(kc), kc == 0, kc == 15, signal=(kc == 15))
        issue_next()
        ACTF(modrow[t % 2].v((0, 256)), pm, AF.Copy)
        if pend_tr:
            ada_transposes(pend_tr.pop())
        pend_tr.append(t)

    def big_transposes(Tb):
        mr = modrow[Tb % 2]
        for j in range(4):
            col = 4 * Tb + j
            MM(bank(6), mr.v((j * 128, j * 128 + 128)), one11.v(), True, True, signal=(j == 3),
               out_ap=bank(6, col, col + 1).ap)

    big_order = [0, 4, 1, 5, 2, 6, 3, 7]

    def issue_big_seq(k):
        Tb = big_order[k]
        P.dma("pool", "abig%d" % (k % 4), abig[k % 4].v().ap, w_src(wada_d, Tb * 512, 512), writes=[abig[k % 4].v()])

    def ada_big(k):
        Tb = big_order[k]
        w = abig[k % 4]
        pm = bank(4 + k % 2, 0, 512, p=(0, 1))
        for kc in range(16):
            MM(pm, cactT.v((kc, kc + 1)), w.v(kc), kc == 0, kc == 15, signal=(kc == 15))
        if k + 4 < 8:
            issue_big_seq(k + 4)
        elif k in (4, 5):
            issue_next()
            issue_next()
        ACTF(modrow[Tb % 2].v(), pm, AF.Copy)
        big_transposes(Tb)

    def modulate_group(i):
        cs, ss = (4 * i, 4 * i + 4), (16 + 4 * i, 20 + 4 * i)
        STT(modT.v(cs), bank(6), 0.5, badaT.v(cs), ALU.mult, ALU.add, in0_ap=bank(6, cs[0], cs[1]).ap)
        STT(modT.v(ss), bank(6), 0.5, badaT.v(ss), ALU.mult, ALU.add, in0_ap=bank(6, ss[0], ss[1]).ap)
        STT(geff.v(cs), modT.v(ss), 1.0, ngT.v(cs), ALU.add, ALU.mult)
        for c in range(cs[0], cs[1]):
            TS("dve", hT.v(c), hT.v(c), geff.v((c, c + 1)), modT.v((c, c + 1)), ALU.mult, ALU.add)

    x_front(0)
    x_front(1)
    pending_groups = []
    x_done = False
    for k in range(8):
        ada_big(k)
        for b_ in (2 * k, 2 * k + 1):
            if b_ < 9:
                x_back(b_)
        for b_ in (2 * k + 2, 2 * k + 3):
            if b_ < 9:
                x_front(b_)
        if 2 * k + 1 >= 8:
            x_done = True
        if k % 2 == 1:
            pending_groups.append(k // 2)
        if x_done:
            while pending_groups:
                modulate_group(pending_groups.pop(0))

    maybe_stop('A')
    fm_count = [0]

    def fm_tile(w, evac):
        for j in range(2):
            pi = fm_count[0] % 2
            fm_count[0] += 1
            for kc in range(16):
                for hf in range(2):
                    MM(bank(2 * pi + hf), w.v(kc, (j * 128, j * 128 + 128)),
                       hT.v(kc, (128 + hf * 512, 128 + hf * 512 + 512)), kc == 0, kc == 15,
                       signal=(kc == 15 and hf == 1))
            if j == 1:
                issue_next()
            evac(j, pi)

    def silu2_evac(dst, oc, pi, tnh):
        tb = tnh[pi]
        ACTF(tb.v(), pair(pi), AF.Tanh, scale=0.5)
        STT(dst.v(oc), tb.v(), 1.0, pair(pi), ALU.add, ALU.mult)

    for i in range(4):
        w = cur_tile("gs", i)
        fm_tile(w, lambda j, pi, i=i: silu2_evac(ugT, 2 * i + j, pi, tnhC))
        w = cur_tile("u", i)
        fm_tile(w, lambda j, pi, i=i: TT("dve", ugT.v(2 * i + j), pair(pi), ugT.v(2 * i + j), ALU.mult))
        ada_tile(16 + i)
    ada_transposes(pend_tr.pop())
    STT(modT.v((32, 40)), bank(6), 0.5, badaT.v((32, 40)), ALU.mult, ALU.add, in0_ap=bank(6, 32, 40).ap)

    c2cnt = [0]

    def c2_block(b):
        tok = (b * 128, b * 128 + 128)
        for half in range(2):
            bk = c2cnt[0] % 4
            c2cnt[0] += 1
            for gg in range(4):
                g = 4 * half + gg
                dst_ap = bank(bk, gg * 128, gg * 128 + 128).ap
                MM(bank(bk), vn.v(b, (g * 128, g * 128 + 128)), WT.v(g), True, False, signal=False, out_ap=dst_ap)
                for k_, bs in enumerate((bs_hi, bs_lo)):
                    MM(bank(bk), ones1.v(), bs.v((g * 128, g * 128 + 128)), False, k_ == 1,
                       signal=(k_ == 1 and gg == 3), out_ap=dst_ap)
            TT("dve", aT.v((8 + 4 * half, 12 + 4 * half), tok), bank(bk), ugT.v((4 * half, 4 * half + 4), tok),
               ALU.mult, in0_ap=bank3(bk, 4).ap)

    wv = [cur_tile("vs", i) for i in range(4)]

    def ln_finish(b):
        pi = 1 + b % 3
        lt_ = lnt[b % 2]
        TS("dve", lt_.v(), pair(pi), lnst.v(b, (12, 13)), lnst.v(b, (15, 16)), ALU.subtract, ALU.mult)
        TT("dve", lt_.v(), lt_.v(), lng_bc.v(), ALU.mult)
        TT("dve", vn.v(b), lt_.v(), lnb_bc.v(), ALU.add)

    for b in range(NB):
        pi = 1 + b % 3
        tok = (128 + b * 128, 128 + b * 128 + 128)
        for vt in range(4):
            for kc in range(16):
                MM(pair(pi), hT.v(kc, tok), wv[vt].v(kc), kc == 0, kc == 15, signal=(kc == 15 and vt == 3),
                   out_ap=pp[pi][:, vt * 256:vt * 256 + 256])
        for hh in range(2):
            _o = lnst.v(b, (6 * hh, 6 * hh + 6))
            _i = bank(2 * pi + hh)
            P.op("dve", lambda e, o=_o.ap, i=_i.ap: e.bn_stats(out=o, in_=i), reads=[_i], writes=[_o])
        _o = lnst.v(b, (12, 14))
        _i = lnst.v(b, (0, 12))
        P.op("dve", lambda e, o=_o.ap, i=_i.ap: e.bn_aggr(out=o, in_=i), reads=[_i], writes=[_o])
        TS("pool", lnst.v(b, (14, 15)), lnst.v(b, (13, 14)), EPS, None, ALU.add)
        TT("pool", lnst.v(b, (15, 16)), lnst.v(b, (14, 15)), mhalf.v((0, 1)), ALU.pow)
        if b == NB - 1:
            for _ in range(4):
                issue_next()
        if b >= 1:
            ln_finish(b - 1)

    def issue_wo(bt):
        P.dma("pool", "wo%d" % bt, wo_big[bt].v().ap, w_src(wout_d, bt * 512, 512), writes=[wo_big[bt].v()])


    maybe_stop('C1')
    for b in range(NB):
        if b == NB - 2:
            ln_finish(NB - 1)
        c2_block(b)

    maybe_stop('C2')
    for i in range(4):
        w = cur_tile("q", i)

        def evac_q(j, pi, i=i):
            oc = 2 * i + j
            if oc % 2 == 0:
                ACTF(qT.v(oc), pair(pi), AF.Copy, scale=0.125)
            else:
                TS("dve", qT.v(oc), pair(pi), 0.125, None, ALU.mult)
        fm_tile(w, evac_q)

    maybe_stop('B1q')
    w = cur_tile("kv", 0)
    pieces = [(0, 512), (512, 1024), (1024, TH)]
    for kc in range(16):
        for pi_, (a, b_) in enumerate(pieces):
            MM(bank(pi_), w.v(kc, (0, 128)), hT.v(kc, (a, b_)), kc == 0, kc == 15,
               signal=(kc == 15 and pi_ == 2), out_ap=bank(pi_, 0, b_ - a).ap)
    for pi_, (a, b_) in enumerate(pieces):
        for g in range(2):
            src_ap = bank(pi_, 0, b_ - a, p=(g * 64, g * 64 + 64)).ap
            for half in range(2):
                dstv = kdup.v(g, (a, b_), p=(half * 64, half * 64 + 64))
                if pi_ == 0 or (pi_ == 2 and g == 0):
                    ACTF(dstv, bank(pi_), AF.Copy, in_ap=src_ap)
                else:
                    CP("dve", dstv, bank(pi_), in_ap=src_ap)

    maybe_stop('B1k')
    MEMSET("pool", Vpad.v(), 0.0)
    for b in range(9):
        pv = bank(4 + b % 2)
        pv_ap = bank(4 + b % 2, 0, 128).ap
        for kc in range(16):
            MM(pv, hT.v(kc, (b * 128, b * 128 + 128)), w.v(kc, (128, 256)), kc == 0, kc == 15, signal=(kc == 15),
               out_ap=pv_ap)
        pv3 = pv_ap.rearrange("p (g d) -> p g d", g=2)
        TS("dve", Vpad.v(b, None, 0, (0, 64)), pv, 0.5, None, ALU.mult, in0_ap=pv3)
        ACTF(Vpad.v(b, None, 1, (64, 128)), pv, AF.Copy, scale=0.5, in_ap=pv3)
    issue_next()

    maybe_stop('B1v')
    for i in range(4):
        w = cur_tile("ga", i)
        fm_tile(w, lambda j, pi, i=i: silu2_evac(gateT, 2 * i + j, pi, tnhB))
        ada_tile(20 + i)
    ada_transposes(pend_tr.pop())
    STT(modT.v((40, 48)), bank(6), 0.5, badaT.v((40, 48)), ALU.mult, ALU.add, in0_ap=bank(6, 40, 48).ap)

    maybe_stop('B1')
    for bt in (2, 3, 0, 1):
        issue_wo(bt)
    def att_stage1(unit):
        n, g = unit // 2, unit % 2
        u2 = unit % 2
        qtok = (n * 128, n * 128 + 128)
        for kcx in range(2):
            ktok = ((n + kcx) * 128, (n + kcx) * 128 + 128)
            sp_ = pair(kcx)
            mi = 2 if kcx == 1 else (0 if n == 0 else 1)
            for h in range(8):
                c = 4 * g + h // 2
                rows = ((h % 2) * 64, (h % 2) * 64 + 64)
                sl = (h % 2) * 4 + h // 2
                MM(sp_, kdup.v(g, ktok, p=rows), qT.v(c, qtok, p=rows), True, True, signal=(h == 7),
                   out_ap=pp[kcx][:, sl * 128:sl * 128 + 128])
            pt = PT[2 * u2 + kcx]
            ACTF(pt.v(), sp_, AF.Exp, in_ap=sp_.ap.rearrange("p (h q) -> p h q", h=8))
            TT("dve", pt.v(), pt.v(), masks.v(mi), ALU.mult,
               in1_ap=masks.v(mi).ap.unsqueeze(1).broadcast_to([128, 8, 128]))

    def att_stage2(unit):
        n, g = unit // 2, unit % 2
        u2 = unit % 2
        qtok = (n * 128, n * 128 + 128)
        pts = [PT[2 * u2], PT[2 * u2 + 1]]
        def slots(kcx, par):
            v_ = pts[kcx].v((par * 4, par * 4 + 4))
            return v_, v_.ap.rearrange("p a b -> p (a b)")

        denb_ = bank(6 + u2)
        first = True
        for kcx in range(2):
            for par in range(2):
                v_, flat = slots(kcx, par)
                MM(denb_, ones_lh.v(par), v_, first, False, signal=False, rhs_ap=flat)
                first = False
        for pc in range(4):
            c = 4 * g + pc
            dst_ap = bank(6 + u2, pc * 128, pc * 128 + 128).ap
            for k_, es in enumerate((es_hi, es_lo)):
                lastd = (pc == 3 and k_ == 1)
                MM(denb_, es.v((c * 128, c * 128 + 128)), ones1.v(), False, lastd, signal=lastd, out_ap=dst_ap)
        pob_ = bank(4 + u2)
        first = True
        for kcx in range(2):
            for par in range(2):
                v_, flat = slots(kcx, par)
                lastp = (kcx == 1 and par == 1)
                MM(pob_, Vpad.v(n + kcx, g, par), v_, first, lastp, signal=lastp, rhs_ap=flat)
                first = False
        CP_den = bank3(6 + u2, 4)
        _r = rden[u2].v().ap
        _d = CP_den.ap
        ACTF(rden[u2].v(), CP_den, AF.Ln)
        ACTF(rden[u2].v(), rden[u2].v(), AF.Exp, scale=-1.0)
        TT("dve", g2[u2].v(), gateT.v((4 * g, 4 * g + 4), qtok), rden[u2].v(), ALU.mult)
        TT("dve", aT.v((4 * g, 4 * g + 4), qtok), bank(4 + u2), g2[u2].v(), ALU.mult, in0_ap=bank3(4 + u2, 4).ap)

    NU = 2 * NB
    for unit in range(NU + 1):
        if unit < NU:
            att_stage1(unit)
        if unit >= 1:
            att_stage2(unit - 1)

    maybe_stop('B2')
    P.dma("sp", "cst_fg", fg_bc.v().ap, fg_d.partition_broadcast(128), writes=[fg_bc.v()])
    for c in range(16):
        ACTF(Rg.v(c), identf.v(), AF.Copy, scale=modT.v((32 + c, 33 + c)))
    for j in range(4):
        bk = j % 2
        MM(bank(bk), onesf.v(), Rg.v((4 * j, 4 * j + 4)), True, True, signal=True,
           rhs_ap=Rg.v((4 * j, 4 * j + 4)).ap.rearrange("p a b -> p (a b)"))
        ACTF(gate_bc.v((j * 512, j * 512 + 512)), bank(bk), AF.Copy)

    maybe_stop('G')
    def d_finish(b):
        s = b % 2
        col = 16 + b
        ACTF(od[s].v(), xd[s].v(), AF.Square, accum=ssq.v((col, col + 1)))
        rstd_ops(ssq, rstd, col)
        STT(od[s].v(), xd[s].v(), rstd.v((col, col + 1)), fg_bc.v(), ALU.mult, ALU.mult)
        P.dma("sp", "od%d" % s, y_d[b * 128:(b + 1) * 128, :], od[s].v().ap, reads=[od[s].v()], final=True)

    for b in range(NB):
        s = b % 2
        tok = (b * 128, b * 128 + 128)
        P.dma("sp", "xd%d" % s, xd[s].v().ap, x_d[128 + b * 128:128 + (b + 1) * 128, :], writes=[xd[s].v()])
        for kc in range(16):
            for bt in range(4):
                MM(bank(4 * s + bt), aT.v(kc, tok), wo_big[bt].v(kc), kc == 0, kc == 15, signal=(kc == 15))
        for bt in range(4):
            cs = (bt * 512, bt * 512 + 512)
            tm = tmpD[bt % 2]
            TT("dve", tm.v(), bank(4 * s + bt), gate_bc.v(cs), ALU.mult)
            TT("dve", xd[s].v(cs), xd[s].v(cs), tm.v(), ALU.add)
        if b >= 1:
            d_finish(b - 1)
    d_finish(NB - 1)

    assert tile_pos[0] == NT_RING and issued[0] == NT_RING, (tile_pos[0], issued[0], NT_RING)


_MASKS = None


def _masks(has_prev):
    j = np.arange(128)[:, None]
    i = np.arange(128)[None, :]
    mp = (j > i).astype(np.float32)
    mc = (j <= i).astype(np.float32)
    m0 = mp * (1.0 if has_prev else 0.0)
    return np.ascontiguousarray(np.stack([m0, mp, mc], axis=1).reshape(128, 3 * 128))


def kernel(x, c, norm_g, w_ada, b_ada, w_in, attn_sinks, sgu_ln_g, sgu_ln_b, sgu_w, sgu_b, w_out, final_g):
    x = np.asarray(x, np.float32)
    B, S, _ = x.shape
    shared = {
        "norm_g": np.ascontiguousarray(np.asarray(norm_g, np.float32)[0]),
        "w_ada": np.ascontiguousarray(np.asarray(w_ada, np.float32)[0]),
        "b_ada": np.ascontiguousarray(np.asarray(b_ada, np.float32)[0]),
        "w_in": np.ascontiguousarray(np.asarray(w_in, np.float32)[0]),
        "sinks": np.ascontiguousarray(np.asarray(attn_sinks, np.float32)[0]),
        "ln_g": np.ascontiguousarray(np.asarray(sgu_ln_g, np.float32)[0]),
        "ln_b": np.ascontiguousarray(np.asarray(sgu_ln_b, np.float32)[0]),
        "sgu_w": np.ascontiguousarray(np.asarray(sgu_w, np.float32)[0]),
        "sgu_b": np.ascontiguousarray(np.asarray(sgu_b, np.float32)[0].reshape(-1)),
        "w_out": np.ascontiguousarray(np.asarray(w_out, np.float32)[0]),
        "final_g": np.ascontiguousarray(np.asarray(final_g, np.float32)),
    }
    c = np.asarray(c, np.float32)
    in_maps = []
    for j in range(NCORES):
        b, half = j // 2, j % 2
        xs = np.zeros((TH, D), np.float32)
        xs[128:] = x[b, half * T:(half + 1) * T]
        if half == 1:
            xs[:128] = x[b, T - 128:T]
        m = dict(shared)
        m["x"] = xs
        m["c"] = np.ascontiguousarray(c[b])
        m["masks"] = _masks(half == 1)
        in_maps.append(m)
    nc = build_nc()
    res = run_bass_kernel_spmd(nc, in_maps, core_ids=list(range(NCORES)))
    out = np.empty((B, S, D), np.float32)
    for j in range(NCORES):
        b, half = j // 2, j % 2
        out[b, half * T:(half + 1) * T] = res.results[j]["y"]
    return out
```

```python
import numpy as np
import concourse.bass as bass
import concourse.mybir as mybir
from concourse.bass_utils import run_bass_kernel_spmd

F32 = mybir.dt.float32
BF16 = mybir.dt.bfloat16
ALU = mybir.AluOpType
AF = mybir.ActivationFunctionType

D = 2048
KC = 16
T = 1024
TH = T + 128
NB = 8
D_IN = 5376
EPS = 1e-6
NCORES = 8


class V:
    __slots__ = ("ap", "rng")

    def __init__(self, ap, rng):
        self.ap = ap
        self.rng = rng


class LT:
    def __init__(self, key, base_ap, byte_off, dims, dtype):
        self.key = key
        self.off = byte_off
        self.dims = list(dims)
        self.ds = 2 if dtype == BF16 else 4
        if len(dims) == 1:
            self.ap = base_ap
        elif len(dims) == 2:
            self.ap = base_ap.rearrange("p (a b) -> p a b", a=dims[0])
        elif len(dims) == 3:
            self.ap = base_ap.rearrange("p (a b c) -> p a b c", a=dims[0], b=dims[1])
        elif len(dims) == 4:
            self.ap = base_ap.rearrange("p (a b c d) -> p a b c d", a=dims[0], b=dims[1], c=dims[2])
        else:
            raise ValueError

    def v(self, *idx, p=None):
        idx = list(idx) + [None] * (len(self.dims) - len(idx))
        key = [slice(None) if p is None else slice(p[0], p[1])]
        norm = []
        for i, d in zip(idx, self.dims):
            if i is None:
                key.append(slice(None)); norm.append((0, d))
            elif isinstance(i, int):
                key.append(i); norm.append((i, i + 1))
            else:
                key.append(slice(i[0], i[1])); norm.append((i[0], i[1]))
        ap = self.ap[tuple(key)]
        n = len(self.dims)
        strides = [1] * n
        for k in range(n - 2, -1, -1):
            strides[k] = strides[k + 1] * self.dims[k + 1]
        k = n - 1
        while k > 0 and norm[k] == (0, self.dims[k]):
            k -= 1
        outer = [range(a, b) for (a, b) in norm[:k]]
        cnt = 1
        for r in outer:
            cnt *= len(r)
        rng = []
        if cnt > 48:
            lo = sum(norm[j][0] * strides[j] for j in range(n))
            hi = sum((norm[j][1] - 1) * strides[j] for j in range(n)) + 1
            rng.append((self.key, self.off + lo * self.ds, self.off + hi * self.ds))
        else:
            import itertools
            for combo in itertools.product(*outer):
                base = sum(c * strides[j] for j, c in enumerate(combo))
                lo = base + norm[k][0] * strides[k]
                hi = base + norm[k][1] * strides[k]
                rng.append((self.key, self.off + lo * self.ds, self.off + hi * self.ds))
        return V(ap, rng)


class Prog:
    ENG = ("pe", "act", "dve", "pool", "sp")

    def __init__(self, nc):
        self.nc = nc
        self.q = {e: [] for e in self.ENG}
        self.sems = {}
        self.cnt = {}
        for e in ("pe", "act", "dve", "pool"):
            self.sems[e] = nc.alloc_semaphore("s_" + e)
            self.cnt[e] = 0
        self.seen = {e: {} for e in self.ENG}
        self.recs = {}
        self.pe_pending = False
        self.final_waits = []

    def _deps(self, reads, writes, eng=None):
        waits = []
        for v in reads:
            for (key, lo, hi) in v.rng:
                for r in self.recs.get(key, ()):
                    if (r[3] or (key == "psum" and r[4] != eng)) and r[0] < hi and lo < r[1]:
                        waits.append(r[2])
        for v in writes:
            for (key, lo, hi) in v.rng:
                for r in self.recs.get(key, ()):
                    if r[0] < hi and lo < r[1]:
                        waits.append(r[2])
        return waits

    def _record(self, eng, ticket, reads, writes):
        for v in writes:
            for (key, lo, hi) in v.rng:
                lst = self.recs.setdefault(key, [])
                lst[:] = [r for r in lst if not (lo <= r[0] and r[1] <= hi)]
                lst.append((lo, hi, ticket, True, eng))
        for v in reads:
            for (key, lo, hi) in v.rng:
                lst = self.recs.setdefault(key, [])
                for i, r in enumerate(lst):
                    if (not r[3]) and r[4] == eng and r[0] == lo and r[1] == hi:
                        lst[i] = (lo, hi, ticket, False, eng)
                        break
                else:
                    lst.append((lo, hi, ticket, False, eng))

    def _emit_waits(self, eng, waits):
        best = {}
        for (sk, val, src) in waits:
            if src == "pe" and eng == "pe":
                continue
            if val > best.get(sk, 0):
                best[sk] = val
        for sk, val in best.items():
            if self.seen[eng].get(sk, 0) >= val:
                continue
            self.seen[eng][sk] = val
            self.q[eng].append(("w", sk, val))

    def op(self, eng, fn, reads=(), writes=(), signal=True):
        self._emit_waits(eng, self._deps(reads, writes, eng))
        if signal:
            self.cnt[eng] += 1
            ticket = (eng, self.cnt[eng], eng)
            self.q[eng].append(("i", fn, eng, 1))
            if eng == "pe":
                self.pe_pending = False
        else:
            assert eng == "pe"
            ticket = (eng, self.cnt[eng] + 1, eng)
            self.q[eng].append(("i", fn, None, 0))
            self.pe_pending = True
        self._record(eng, ticket, reads, writes)
        return ticket

    def dma(self, eng, semkey, out_ap, in_ap, reads=(), writes=(), final=False, noncontig=False):
        if semkey not in self.sems:
            self.sems[semkey] = self.nc.alloc_semaphore("d_" + semkey)
            self.cnt[semkey] = 0
        self._emit_waits(eng, self._deps(reads, writes))
        self.cnt[semkey] += 16
        ticket = (semkey, self.cnt[semkey], "dma")
        if noncontig:
            def fn(e, o=out_ap, i=in_ap):
                with self.nc.allow_non_contiguous_dma(reason="small strided param load"):
                    return e.dma_start(out=o, in_=i)
        else:
            def fn(e, o=out_ap, i=in_ap):
                return e.dma_start(out=o, in_=i)
        self.q[eng].append(("i", fn, semkey, 16))
        self._record("dma", ticket, reads, writes)
        if final:
            self.final_waits.append(ticket)
        return ticket

    def finish(self):
        assert not self.pe_pending
        self._emit_waits("sp", self.final_waits)
        nc = self.nc
        sems = self.sems
        q = self.q

        def run(e, items):
            for it in items:
                if it[0] == "w":
                    e.wait_ge(sems[it[1]], it[2])
                else:
                    ins = it[1](e)
                    if it[2] is not None:
                        ins.then_inc(sems[it[2]], it[3])

        with nc.Block() as block:
            @block.sync
            def _(e):
                run(e, q["sp"])

            @block.gpsimd
            def _(e):
                run(e, q["pool"])

            @block.scalar
            def _(e):
                run(e, q["act"])

            @block.vector
            def _(e):
                run(e, q["dve"])

            @block.tensor
            def _(e):
                run(e, q["pe"])


class _Stop(Exception):
    pass


def build_nc(stop=None):
    nc = bass.Bass("TRN2", target_bir_lowering=False)
    P = Prog(nc)
    try:
        _build_body(nc, P, stop)
    except _Stop:
        pass
    P.finish()
    return nc


def _build_body(nc, P, stop):
    def maybe_stop(tag):
        if stop == tag:
            raise _Stop()

    def din(name, shape):
        return nc.dram_tensor(name, list(shape), F32, kind="ExternalInput").ap()

    x_d = din("x", [TH, D])
    c_d = din("c", [D])
    ng_d = din("norm_g", [D])
    wada_d = din("w_ada", [D, 3 * D])
    bada_d = din("b_ada", [3 * D])
    win_d = din("w_in", [D, D_IN])
    sink_d = din("sinks", [16])
    lng_d = din("ln_g", [1024])
    lnb_d = din("ln_b", [1024])
    sw_d = din("sgu_w", [8, 128, 128])
    sb_d = din("sgu_b", [1024])
    wout_d = din("w_out", [D, D])
    fg_d = din("final_g", [D])
    mask_d = din("masks", [128, 3 * 128])
    y_d = nc.dram_tensor("y", [T, D], F32, kind="ExternalOutput").ap()

    def arena(name, nbytes):
        return nc.alloc_sbuf_tensor(name, [128, nbytes // 4], F32)

    def lt_in(ar, key, off, dims, dtype):
        n = 1
        for d_ in dims:
            n *= d_
        ds = 2 if dtype == BF16 else 4
        base = ar[:, off // 4:(off + n * ds) // 4]
        if dtype == BF16:
            base = base.bitcast(BF16)
        return LT(key, base, off, dims, dtype)

    def own(name, dims, dtype, parts=128):
        n = 1
        for d_ in dims:
            n *= d_
        h = nc.alloc_sbuf_tensor("sb_" + name, [parts, n], dtype)
        return LT(name, h[:, :], 0, dims, dtype)

    aT_h = nc.alloc_sbuf_tensor("sb_aT", [128, 16 * T], BF16)
    aT = LT("aT", aT_h[:, :], 0, [16, T], BF16)
    aT_f32 = aT_h[:, :].bitcast(F32)

    def aT_tmp(off, dims):
        n = 1
        for d_ in dims:
            n *= d_
        return LT("aT", aT_f32[:, off // 4:(off + 4 * n) // 4], off, dims, F32)
    ring_ar = arena("ring", 32768)
    ring = [lt_in(ring_ar, "ring", s * 8192, [16, 256], BF16) for s in range(4)]
    wo_big = [lt_in(ring_ar, "ring", s * 16384, [16, 512], BF16) for s in range(2)]
    hT_ar = arena("hTa", 36864)
    hT = lt_in(hT_ar, "hTa", 0, [16, TH], BF16)
    wo_big += [lt_in(hT_ar, "hTa", s * 16384, [16, 512], BF16) for s in range(2)]
    S = arena("S", 71680)

    def s_lt(off, dims, dtype):
        return lt_in(S, "S", off, dims, dtype)

    xs = [s_lt(0, [D], F32), s_lt(8192, [D], F32)]
    xn = [s_lt(16384, [D], BF16), s_lt(20480, [D], BF16)]
    sq = s_lt(24576, [D], BF16)
    Wnat = aT_tmp(0, [8, 128])
    Wtmp = aT_tmp(4096, [8, 128])
    rowtmp = aT_tmp(8192, [3, 1024])
    abig = [lt_in(ring_ar, "ring", 0, [16, 512], BF16), lt_in(ring_ar, "ring", 16384, [16, 512], BF16),
            s_lt(28672, [16, 512], BF16), s_lt(45056, [16, 512], BF16)]
    qT = s_lt(0, [8, T], BF16)
    gateT = s_lt(16384, [8, T], BF16)
    kdup = s_lt(32768, [2, TH], BF16)
    Vpad = s_lt(37376, [9, 2, 2, 128], BF16)
    PT = [s_lt(46592 + i * 2048, [8, 128], BF16) for i in range(4)]
    rden = [s_lt(54784 + i * 2048, [4, 128], F32) for i in range(2)]
    g2 = [s_lt(58880 + i * 2048, [4, 128], F32) for i in range(2)]
    tnhB = [s_lt(62976 + i * 4096, [T], F32) for i in range(2)]
    ugT = s_lt(0, [8, T], BF16)
    vn = s_lt(16384, [8, 1024], BF16)
    lnt = [s_lt(32768 + i * 4096, [1024], F32) for i in range(2)]
    tnhC = [s_lt(40960 + i * 4096, [T], F32) for i in range(2)]
    xd = [s_lt(0, [D], F32), s_lt(8192, [D], F32)]
    od = [s_lt(16384, [D], F32), s_lt(24576, [D], F32)]
    tmpD = [s_lt(32768 + i * 2048, [512], F32) for i in range(2)]
    Rg = s_lt(40960, [16, 128], F32)
    gate_bc = s_lt(49152, [D], F32)
    fg_bc = s_lt(57344, [D], F32)

    ident = own("ident", [128], BF16)
    identf = own("identf", [128], F32)
    onesf = own("onesf", [128], F32)
    masks = own("masks", [3, 128], BF16)
    WT = own("WT", [8, 128], BF16)
    maskb = own("maskb", [3, 4, 128], BF16)
    ones_lh = own("ones_lh", [2, 128], BF16)
    lng_bc = own("lng_bc", [1024], F32)
    lnb_bc = own("lnb_bc", [1024], F32)
    cT = own("cT", [16], F32)
    c16 = own("c16", [128], F32, parts=16)
    b48 = own("b48", [128], F32, parts=48)
    ng16 = own("ng16", [128], F32, parts=16)
    tT = own("tT", [16], F32)
    cactT = own("cactT", [16], BF16)
    badaT = own("badaT", [48], F32)
    ngT = own("ngT", [16], F32)
    modT = own("modT", [48], F32)
    geff = own("geff", [16], F32)
    ssq = own("ssq", [32], F32)
    rtmp = own("rtmp", [32], F32)
    rstd = own("rstd", [32], F32)
    mhalf = own("mhalf", [32], F32)
    lnst = own("lnst", [8, 16], F32)
    sink16 = own("sink16", [16], F32, parts=1)
    es16 = own("es16", [16], F32, parts=1)
    es_hi = own("es_hi", [1024], BF16, parts=1)
    es_lo = own("es_lo", [1024], BF16, parts=1)
    bs_hi = own("bs_hi", [1024], BF16, parts=1)
    bs_lo = own("bs_lo", [1024], BF16, parts=1)
    modrow = [own("modrow0", [512], F32, parts=1), own("modrow1", [512], F32, parts=1)]
    ones1 = own("ones1", [128], BF16, parts=1)
    one11 = own("one11", [1], F32, parts=1)

    pp = [nc.alloc_psum_tensor("pp%d" % i, [128, 1024], F32) for i in range(4)]

    def bank(i, a=0, b=512, p=None):
        ps = slice(None) if p is None else slice(p[0], p[1])
        return V(pp[i // 2][ps, (i % 2) * 512 + a:(i % 2) * 512 + b], [("psum", i, i + 1)])

    def bank3(i, n):
        return V(pp[i // 2][:, (i % 2) * 512:(i % 2 + 1) * 512].rearrange("p (a b) -> p a b", a=n),
                 [("psum", i, i + 1)])

    def pair(i):
        return V(pp[i][:, :], [("psum", 2 * i, 2 * i + 2)])

    def pair_bf(i):
        return V(pp[i][:, :].bitcast(BF16).rearrange("p (c t) -> p c t", t=128),
                 [("psum", 2 * i, 2 * i + 2)])


    def vs_(x):
        return [x] if isinstance(x, V) else []

    def sc_(x):
        return x.ap if isinstance(x, V) else x

    def MM(out, lhs, rhs, start, stop, signal, rhs_ap=None, out_ap=None):
        o = out.ap if out_ap is None else out_ap
        l = lhs.ap
        r = rhs.ap if rhs_ap is None else rhs_ap
        P.op("pe", lambda e: e.matmul(o, lhsT=l, rhs=r, start=start, stop=stop),
             reads=[lhs, rhs], writes=[out], signal=signal)

    def TR(out, in_, idv, signal, out_ap=None):
        o = out.ap if out_ap is None else out_ap
        i = in_.ap
        d = idv.ap
        P.op("pe", lambda e: e.transpose(out=o, in_=i, identity=d), reads=[in_, idv], writes=[out], signal=signal)

    def ACTF(out, in_, func, scale=1.0, accum=None, in_ap=None, out_ap=None, extra_w=()):
        o = out.ap if out_ap is None else out_ap
        i = in_.ap if in_ap is None else in_ap
        kw = {}
        if accum is not None:
            kw["accum_out"] = accum.ap
        sc = sc_(scale)
        P.op("act", lambda e: e.activation(out=o, in_=i, func=func, scale=sc, **kw),
             reads=[in_] + vs_(scale), writes=[out] + vs_(accum) + list(extra_w))

    def TS(eng, out, in0, s1, s2, op0, op1=None, in0_ap=None, out_ap=None):
        o = out.ap if out_ap is None else out_ap
        i = in0.ap if in0_ap is None else in0_ap
        a1, a2 = sc_(s1), sc_(s2)
        kw = {} if op1 is None else {"op1": op1}
        P.op(eng, lambda e: e.tensor_scalar(out=o, in0=i, scalar1=a1, scalar2=a2, op0=op0, **kw),
             reads=[in0] + vs_(s1) + vs_(s2), writes=[out])

    def TT(eng, out, in0, in1, op, in0_ap=None, in1_ap=None, out_ap=None):
        o = out.ap if out_ap is None else out_ap
        i0 = in0.ap if in0_ap is None else in0_ap
        i1 = in1.ap if in1_ap is None else in1_ap
        P.op(eng, lambda e: e.tensor_tensor(out=o, in0=i0, in1=i1, op=op), reads=[in0, in1], writes=[out])

    def STT(out, in0, scalar, in1, op0, op1, in0_ap=None, in1_ap=None):
        o = out.ap
        i0 = in0.ap if in0_ap is None else in0_ap
        i1 = in1.ap if in1_ap is None else in1_ap
        a = sc_(scalar)
        P.op("dve", lambda e: e.scalar_tensor_tensor(out=o, in0=i0, scalar=a, in1=i1, op0=op0, op1=op1),
             reads=[in0, in1] + vs_(scalar), writes=[out])

    def CP(eng, out, in_, in_ap=None, out_ap=None):
        o = out.ap if out_ap is None else out_ap
        i = in_.ap if in_ap is None else in_ap
        P.op(eng, lambda e: e.tensor_copy(out=o, in_=i), reads=[in_], writes=[out])

    def MEMSET(eng, out, val):
        o = out.ap
        P.op(eng, lambda e: e.memset(o, val), writes=[out])

    P.dma("act", "cst_c", c16.v().ap, c_d.rearrange("(k p) -> k p", p=128), writes=[c16.v()])
    P.dma("act", "cst", b48.v().ap, bada_d.rearrange("(k p) -> k p", p=128), writes=[b48.v()])
    P.dma("act", "cst", ng16.v().ap, ng_d.rearrange("(k p) -> k p", p=128), writes=[ng16.v()])
    P.dma("act", "cst", sink16.v().ap, sink_d.unsqueeze(0), writes=[sink16.v()])
    P.dma("act", "cst", rowtmp.v(2, p=(0, 1)).ap, sb_d.unsqueeze(0), writes=[rowtmp.v(2)])
    P.dma("act", "cst", Wnat.v().ap, sw_d.rearrange("g t s -> t g s"), writes=[Wnat.v()])
    P.dma("act", "cst", lng_bc.v().ap, lng_d.partition_broadcast(128), writes=[lng_bc.v()])
    t_cst = P.dma("act", "cst", lnb_bc.v().ap, lnb_d.partition_broadcast(128), writes=[lnb_bc.v()])
    for key in list(P.recs):
        P.recs[key] = [(r[0], r[1], t_cst if r[2][0] == "cst" else r[2], r[3], r[4]) for r in P.recs[key]]

    MEMSET("pool", identf.v(), 1.0)
    _o = identf.v().ap
    P.op("pool", lambda e, _o=_o: e.affine_select(out=_o, in_=_o, pattern=[[-1, 128]], compare_op=ALU.is_equal,
                                          fill=0.0, base=0, channel_multiplier=1),
         reads=[identf.v()], writes=[identf.v()])
    for src, dst, k in ((c16, cT, 16), (b48, badaT, 48), (ng16, ngT, 16)):
        TR(bank(7), src.v(), identf.v((0, k), p=(0, k)), signal=True, out_ap=bank(7, 0, k).ap)
        CP("dve", dst.v(), bank(7), in_ap=bank(7, 0, k).ap)
    P.dma("pool", "cst_m", masks.v().ap, mask_d.rearrange("p (a b) -> p a b", a=3), writes=[masks.v()])

    def w_src(dram, c0, ncols):
        return dram[:, c0:c0 + ncols].rearrange("(k p) n -> p k n", p=128)

    tiles = []
    for i in range(4):
        tiles += [("gs", i), ("ada", 16 + i), ("u", i)]
    tiles += [("vs", i) for i in range(4)]
    tiles += [("q", i) for i in range(4)] + [("kv", 0)]
    for i in range(4):
        tiles += [("ga", i), ("ada", 20 + i)]
    NT_RING = len(tiles)
    issued = [0]
    col0 = {"q": 0, "ga": 1280, "u": 2304, "vs": 3328, "gs": 4352}

    def issue_next():
        i = issued[0]
        if i >= NT_RING:
            return
        issued[0] += 1
        kind, j = tiles[i]
        slot = i % 4
        dst = ring[slot]
        sk = "ring%d" % slot
        if kind == "ada":
            P.dma("pool", sk, dst.v().ap, w_src(wada_d, j * 256, 256), writes=[dst.v()])
        elif kind == "kv":
            P.dma("pool", sk, dst.v().ap, w_src(win_d, 1024, 256), writes=[dst.v()])
        else:
            P.dma("pool", sk, dst.v().ap, w_src(win_d, col0[kind] + j * 256, 256), writes=[dst.v()])

    tile_pos = [0]

    def cur_tile(kind, j):
        i = tile_pos[0]
        assert tiles[i] == (kind, j), (tiles[i], kind, j)
        tile_pos[0] += 1
        return ring[i % 4]

    def issue_big(Tb):
        P.dma("pool", "abig%d" % (Tb % 4), abig[Tb % 4].v().ap, w_src(wada_d, Tb * 512, 512), writes=[abig[Tb % 4].v()])

    for k_ in range(4):
        Tb_ = [0, 4, 1, 5][k_]
        P.dma("pool", "abig%d" % k_, abig[k_].v().ap, w_src(wada_d, Tb_ * 512, 512), writes=[abig[k_].v()])

    MEMSET("pool", onesf.v(), 1.0)
    MEMSET("pool", ones_lh.v(), 0.0)
    MEMSET("pool", ones_lh.v(0, (0, 64)), 1.0)
    MEMSET("pool", ones_lh.v(1, (64, 128)), 1.0)
    MEMSET("pool", mhalf.v(), -0.5)
    MEMSET("pool", ssq.v(), 0.0)
    MEMSET("pool", ones1.v(), 1.0)
    MEMSET("pool", one11.v(), 1.0)
    CP("dve", ident.v(), identf.v())

    ACTF(tT.v(), cT.v(), AF.Tanh, scale=0.5)
    STT(cactT.v(), tT.v(), 1.0, cT.v(), ALU.add, ALU.mult)

    def hilo_row(src, hi, lo, hif):
        CP("dve", hi.v(), src)
        CP("dve", hif, hi.v())
        TT("dve", lo.v(), src, hif, ALU.subtract)

    ACTF(es16.v(), sink16.v(), AF.Exp)
    es_f = rowtmp.v(0, p=(0, 1))
    hif = rowtmp.v(1, p=(0, 1))
    CP("dve", es_f, es16.v(), in_ap=es16.v().ap.unsqueeze(2).broadcast_to([1, 16, 64]),
       out_ap=es_f.ap.rearrange("p (h d) -> p h d", d=64))
    hilo_row(es_f, es_hi, es_lo, hif)
    bs_f = rowtmp.v(2, p=(0, 1))
    TS("dve", bs_f, bs_f, 0.5, None, ALU.mult)
    hilo_row(bs_f, bs_hi, bs_lo, hif)

    for r in range(2):
        for gg in range(4):
            g = 4 * r + gg
            TR(bank(7), Wnat.v(g), identf.v(), signal=(gg == 3), out_ap=bank(7, gg * 128, gg * 128 + 128).ap)
        CP("dve", Wtmp.v((4 * r, 4 * r + 4)), bank(7), in_ap=bank3(7, 4).ap)
    _w = Wtmp.v().ap
    P.op("pool", lambda e, _w=_w: e.affine_select(out=_w, in_=_w, pattern=[[0, 8], [1, 128]], compare_op=ALU.is_ge,
                                          fill=0.0, base=0, channel_multiplier=-1),
         reads=[Wtmp.v()], writes=[Wtmp.v()])
    ACTF(WT.v(), Wtmp.v(), AF.Copy, scale=0.5)

    for mi_ in range(3):
        TS("dve", maskb.v(mi_), masks.v(mi_), -1.0, 30000.0, ALU.add, ALU.mult,
           in0_ap=masks.v(mi_).ap.unsqueeze(1).broadcast_to([128, 4, 128]))
    maybe_stop('setup')
    def rstd_ops(src, dst, col):
        cv = (col, col + 1)
        TS("pool", rtmp.v(cv), src.v(cv), 1.0 / D, EPS, ALU.mult, ALU.add)
        TT("pool", dst.v(cv), rtmp.v(cv), mhalf.v((0, 1)), ALU.pow)

    def x_front(b):
        s = b % 2
        P.dma("sp", "xs%d" % s, xs[s].v().ap, x_d[b * 128:(b + 1) * 128, :], writes=[xs[s].v()])
        ACTF(sq.v(), xs[s].v(), AF.Square, accum=ssq.v((b, b + 1)))
        rstd_ops(ssq, rstd, b)
        TS("dve", xn[s].v(), xs[s].v(), rstd.v((b, b + 1)), None, ALU.mult)

    def x_back(b):
        s = b % 2
        pb = pair_bf(s)
        for c in range(16):
            TR(pb, xn[s].v((c * 128, c * 128 + 128)), ident.v(), signal=(c == 15), out_ap=pb.ap[:, c, :])
        tk = (b * 128, b * 128 + 128)
        ACTF(hT.v((0, 8), tk), bank(2 * s), AF.Copy, in_ap=pb.ap[:, 0:8, :])
        CP("dve", hT.v((8, 16), tk), bank(2 * s + 1), in_ap=pb.ap[:, 8:16, :])

    pend_tr = []

    def ada_transposes(t):
        mr = modrow[t % 2]
        for j in range(2):
            col = 2 * t + j
            MM(bank(6), mr.v((j * 128, j * 128 + 128)), one11.v(), True, True, signal=(j == 1),
               out_ap=bank(6, col, col + 1).ap)

    def ada_tile(t):
        w = cur_tile("ada", t)
        pm = bank(4 + t % 2, 0, 256, p=(0, 1))
        for kc in range(16):
            MM(pm, cactT.v((kc, kc + 1)), w.v(kc), kc == 0, kc == 15, signal=(kc == 15))
        issue_next()
        ACTF(modrow[t % 2].v((0, 256)), pm, AF.Copy)
        if pend_tr:
            ada_transposes(pend_tr.pop())
        pend_tr.append(t)

    def big_transposes(Tb):
        mr = modrow[Tb % 2]
        for j in range(4):
            col = 4 * Tb + j
            MM(bank(6), mr.v((j * 128, j * 128 + 128)), one11.v(), True, True, signal=(j == 3),
               out_ap=bank(6, col, col + 1).ap)

    big_order = [0, 4, 1, 5, 2, 6, 3, 7]

    def issue_big_seq(k):
        Tb = big_order[k]
        P.dma("pool", "abig%d" % (k % 4), abig[k % 4].v().ap, w_src(wada_d, Tb * 512, 512), writes=[abig[k % 4].v()])

    def ada_big(k):
        Tb = big_order[k]
        w = abig[k % 4]
        pm = bank(4 + k % 2, 0, 512, p=(0, 1))
        for kc in range(16):
            MM(pm, cactT.v((kc, kc + 1)), w.v(kc), kc == 0, kc == 15, signal=(kc == 15))
        if k + 4 < 8:
            issue_big_seq(k + 4)
        elif k in (4, 5):
            issue_next()
            issue_next()
        ACTF(modrow[Tb % 2].v(), pm, AF.Copy)
        big_transposes(Tb)

    def modulate_group(i):
        cs, ss = (4 * i, 4 * i + 4), (16 + 4 * i, 20 + 4 * i)
        STT(modT.v(cs), bank(6), 0.5, badaT.v(cs), ALU.mult, ALU.add, in0_ap=bank(6, cs[0], cs[1]).ap)
        STT(modT.v(ss), bank(6), 0.5, badaT.v(ss), ALU.mult, ALU.add, in0_ap=bank(6, ss[0], ss[1]).ap)
        STT(geff.v(cs), modT.v(ss), 1.0, ngT.v(cs), ALU.add, ALU.mult)
        for c in range(cs[0], cs[1]):
            TS("dve", hT.v(c), hT.v(c), geff.v((c, c + 1)), modT.v((c, c + 1)), ALU.mult, ALU.add)

    x_front(0)
    x_front(1)
    pending_groups = []
    x_done = False
    for k in range(8):
        ada_big(k)
        for b_ in (2 * k, 2 * k + 1):
            if b_ < 9:
                x_back(b_)
        for b_ in (2 * k + 2, 2 * k + 3):
            if b_ < 9:
                x_front(b_)
        if 2 * k + 1 >= 8:
            x_done = True
        if k % 2 == 1:
            pending_groups.append(k // 2)
        if x_done:
            while pending_groups:
                modulate_group(pending_groups.pop(0))

    maybe_stop('A')
    fm_count = [0]

    def fm_tile(w, evac):
        for j in range(2):
            pi = fm_count[0] % 2
            fm_count[0] += 1
            for kc in range(16):
                for hf in range(2):
                    MM(bank(2 * pi + hf), w.v(kc, (j * 128, j * 128 + 128)),
                       hT.v(kc, (128 + hf * 512, 128 + hf * 512 + 512)), kc == 0, kc == 15,
                       signal=(kc == 15 and hf == 1))
            if j == 1:
                issue_next()
            evac(j, pi)

    def silu2_evac(dst, oc, pi, tnh):
        tb = tnh[pi]
        ACTF(tb.v(), pair(pi), AF.Tanh, scale=0.5)
        STT(dst.v(oc), tb.v(), 1.0, pair(pi), ALU.add, ALU.mult)

    for i in range(4):
        w = cur_tile("gs", i)
        fm_tile(w, lambda j, pi, i=i: silu2_evac(ugT, 2 * i + j, pi, tnhC))
        ada_tile(16 + i)
        w = cur_tile("u", i)
        fm_tile(w, lambda j, pi, i=i: TT("dve", ugT.v(2 * i + j), pair(pi), ugT.v(2 * i + j), ALU.mult))
    ada_transposes(pend_tr.pop())
    STT(modT.v((32, 40)), bank(6), 0.5, badaT.v((32, 40)), ALU.mult, ALU.add, in0_ap=bank(6, 32, 40).ap)

    c2cnt = [0]

    def c2_block(b):
        tok = (b * 128, b * 128 + 128)
        for half in range(2):
            bk = c2cnt[0] % 4
            c2cnt[0] += 1
            for gg in range(4):
                g = 4 * half + gg
                dst_ap = bank(bk, gg * 128, gg * 128 + 128).ap
                MM(bank(bk), vn.v(b, (g * 128, g * 128 + 128)), WT.v(g), True, False, signal=False, out_ap=dst_ap)
                for k_, bs in enumerate((bs_hi, bs_lo)):
                    MM(bank(bk), ones1.v(), bs.v((g * 128, g * 128 + 128)), False, k_ == 1,
                       signal=(k_ == 1 and gg == 3), out_ap=dst_ap)
            TT("dve", aT.v((8 + 4 * half, 12 + 4 * half), tok), bank(bk), ugT.v((4 * half, 4 * half + 4), tok),
               ALU.mult, in0_ap=bank3(bk, 4).ap)

    wv = [cur_tile("vs", i) for i in range(4)]

    def ln_finish(b):
        pi = 1 + b % 3
        lt_ = lnt[b % 2]
        TS("dve", lt_.v(), pair(pi), lnst.v(b, (12, 13)), lnst.v(b, (15, 16)), ALU.subtract, ALU.mult)
        TT("dve", lt_.v(), lt_.v(), lng_bc.v(), ALU.mult)
        TT("dve", vn.v(b), lt_.v(), lnb_bc.v(), ALU.add)

    for b in range(NB):
        pi = 1 + b % 3
        tok = (128 + b * 128, 128 + b * 128 + 128)
        for vt in range(4):
            for kc in range(16):
                MM(pair(pi), hT.v(kc, tok), wv[vt].v(kc), kc == 0, kc == 15, signal=(kc == 15 and vt == 3),
                   out_ap=pp[pi][:, vt * 256:vt * 256 + 256])
        for hh in range(2):
            _o = lnst.v(b, (6 * hh, 6 * hh + 6))
            _i = bank(2 * pi + hh)
            P.op("dve", lambda e, o=_o.ap, i=_i.ap: e.bn_stats(out=o, in_=i), reads=[_i], writes=[_o])
        _o = lnst.v(b, (12, 14))
        _i = lnst.v(b, (0, 12))
        P.op("dve", lambda e, o=_o.ap, i=_i.ap: e.bn_aggr(out=o, in_=i), reads=[_i], writes=[_o])
        TS("pool", lnst.v(b, (14, 15)), lnst.v(b, (13, 14)), EPS, None, ALU.add)
        TT("pool", lnst.v(b, (15, 16)), lnst.v(b, (14, 15)), mhalf.v((0, 1)), ALU.pow)
        if b == NB - 1:
            for _ in range(4):
                issue_next()
        if b >= 1:
            ln_finish(b - 1)

    def issue_wo(bt):
        P.dma("pool", "wo%d" % bt, wo_big[bt].v().ap, w_src(wout_d, bt * 512, 512), writes=[wo_big[bt].v()])


    maybe_stop('C1')
    for b in range(NB):
        if b == NB - 2:
            ln_finish(NB - 1)
        c2_block(b)

    maybe_stop('C2')
    for i in range(4):
        w = cur_tile("q", i)

        def evac_q(j, pi, i=i):
            oc = 2 * i + j
            if oc % 2 == 0:
                ACTF(qT.v(oc), pair(pi), AF.Copy, scale=0.125)
            else:
                TS("dve", qT.v(oc), pair(pi), 0.125, None, ALU.mult)
        fm_tile(w, evac_q)

    maybe_stop('B1q')
    w = cur_tile("kv", 0)
    MEMSET("pool", Vpad.v(), 0.0)
    for b in range(9):
        pv = bank(4 + b % 2)
        pv_ap = bank(4 + b % 2, 0, 128).ap
        for kc in range(16):
            MM(pv, hT.v(kc, (b * 128, b * 128 + 128)), w.v(kc, (128, 256)), kc == 0, kc == 15, signal=(kc == 15),
               out_ap=pv_ap)
        pv3 = pv_ap.rearrange("p (g d) -> p g d", g=2)
        TS("dve", Vpad.v(b, None, 0, (0, 64)), pv, 0.5, None, ALU.mult, in0_ap=pv3)
        ACTF(Vpad.v(b, None, 1, (64, 128)), pv, AF.Copy, scale=0.5, in_ap=pv3)

    maybe_stop('B1k')
    pieces = [(0, 512, 0), (512, 1024, 1), (1024, TH, 4)]
    for kc in range(16):
        for pi_, (a, b_, bk_) in enumerate(pieces):
            MM(bank(bk_), w.v(kc, (0, 128)), hT.v(kc, (a, b_)), kc == 0, kc == 15,
               signal=(kc == 15 and pi_ == 2), out_ap=bank(bk_, 0, b_ - a).ap)
    issue_next()
    for pi_, (a, b_, bk_) in enumerate(pieces):
        for g in range(2):
            src_ap = bank(bk_, 0, b_ - a, p=(g * 64, g * 64 + 64)).ap
            for half in range(2):
                dstv = kdup.v(g, (a, b_), p=(half * 64, half * 64 + 64))
                if pi_ == 0 or (pi_ == 2 and g == 0):
                    ACTF(dstv, bank(bk_), AF.Copy, in_ap=src_ap)
                else:
                    CP("dve", dstv, bank(bk_), in_ap=src_ap)
    if fm_count[0] % 2 == 0:
        fm_count[0] += 1

    maybe_stop('B1v')
    for i in range(4):
        w = cur_tile("ga", i)
        fm_tile(w, lambda j, pi, i=i: silu2_evac(gateT, 2 * i + j, pi, tnhB))
        ada_tile(20 + i)
    ada_transposes(pend_tr.pop())
    STT(modT.v((40, 48)), bank(6), 0.5, badaT.v((40, 48)), ALU.mult, ALU.add, in0_ap=bank(6, 40, 48).ap)

    maybe_stop('B1')
    for bt in (2, 3, 0, 1):
        issue_wo(bt)
    def att_stage1(unit):
        n, g = unit // 2, unit % 2
        u2 = unit % 2
        qtok = (n * 128, n * 128 + 128)
        for kcx in range(2):
            ktok = ((n + kcx) * 128, (n + kcx) * 128 + 128)
            sp_ = pair(kcx)
            mi = 2 if kcx == 1 else (0 if n == 0 else 1)
            for h in range(8):
                c = 4 * g + h // 2
                rows = ((h % 2) * 64, (h % 2) * 64 + 64)
                sl = (h % 2) * 4 + h // 2
                MM(sp_, kdup.v(g, ktok, p=rows), qT.v(c, qtok, p=rows), True, True, signal=(h == 7),
                   out_ap=pp[kcx][:, sl * 128:sl * 128 + 128])
            pt = PT[2 * u2 + kcx]
            ACTF(pt.v(), sp_, AF.Exp, in_ap=sp_.ap.rearrange("p (h q) -> p h q", h=8))
            TT("dve", pt.v(), pt.v(), masks.v(mi), ALU.mult,
               in1_ap=masks.v(mi).ap.unsqueeze(1).broadcast_to([128, 8, 128]))

    def att_stage2(unit):
        n, g = unit // 2, unit % 2
        u2 = unit % 2
        qtok = (n * 128, n * 128 + 128)
        pts = [PT[2 * u2], PT[2 * u2 + 1]]
        def slots(kcx, par):
            v_ = pts[kcx].v((par * 4, par * 4 + 4))
            return v_, v_.ap.rearrange("p a b -> p (a b)")

        denb_ = bank(6 + u2)
        first = True
        for kcx in range(2):
            for par in range(2):
                v_, flat = slots(kcx, par)
                MM(denb_, ones_lh.v(par), v_, first, False, signal=False, rhs_ap=flat)
                first = False
        for pc in range(4):
            c = 4 * g + pc
            dst_ap = bank(6 + u2, pc * 128, pc * 128 + 128).ap
            for k_, es in enumerate((es_hi, es_lo)):
                lastd = (pc == 3 and k_ == 1)
                MM(denb_, es.v((c * 128, c * 128 + 128)), ones1.v(), False, lastd, signal=lastd, out_ap=dst_ap)
        pob_ = bank(4 + u2)
        first = True
        for kcx in range(2):
            for par in range(2):
                v_, flat = slots(kcx, par)
                lastp = (kcx == 1 and par == 1)
                MM(pob_, Vpad.v(n + kcx, g, par), v_, first, lastp, signal=lastp, rhs_ap=flat)
                first = False
        CP_den = bank3(6 + u2, 4)
        _r = rden[u2].v().ap
        _d = CP_den.ap
        ACTF(rden[u2].v(), CP_den, AF.Ln)
        ACTF(rden[u2].v(), rden[u2].v(), AF.Exp, scale=-1.0)
        TT("dve", g2[u2].v(), gateT.v((4 * g, 4 * g + 4), qtok), rden[u2].v(), ALU.mult)
        TT("dve", aT.v((4 * g, 4 * g + 4), qtok), bank(4 + u2), g2[u2].v(), ALU.mult, in0_ap=bank3(4 + u2, 4).ap)

    NU = 2 * NB
    for unit in range(NU + 1):
        if unit < NU:
            att_stage1(unit)
        if unit >= 1:
            att_stage2(unit - 1)

    maybe_stop('B2')
    P.dma("sp", "cst_fg", fg_bc.v().ap, fg_d.partition_broadcast(128), writes=[fg_bc.v()])
    for c in range(16):
        ACTF(Rg.v(c), identf.v(), AF.Copy, scale=modT.v((32 + c, 33 + c)))
    for j in range(4):
        bk = j % 2
        MM(bank(bk), onesf.v(), Rg.v((4 * j, 4 * j + 4)), True, True, signal=True,
           rhs_ap=Rg.v((4 * j, 4 * j + 4)).ap.rearrange("p a b -> p (a b)"))
        ACTF(gate_bc.v((j * 512, j * 512 + 512)), bank(bk), AF.Copy)

    maybe_stop('G')
    def d_finish(b):
        s = b % 2
        col = 16 + b
        ACTF(od[s].v(), xd[s].v(), AF.Square, accum=ssq.v((col, col + 1)))
        rstd_ops(ssq, rstd, col)
        STT(od[s].v(), xd[s].v(), rstd.v((col, col + 1)), fg_bc.v(), ALU.mult, ALU.mult)
        P.dma("sp", "od%d" % s, y_d[b * 128:(b + 1) * 128, :], od[s].v().ap, reads=[od[s].v()], final=True)

    for b in range(NB):
        s = b % 2
        tok = (b * 128, b * 128 + 128)
        P.dma("sp", "xd%d" % s, xd[s].v().ap, x_d[128 + b * 128:128 + (b + 1) * 128, :], writes=[xd[s].v()])
        for kc in range(16):
            for bt in range(4):
                MM(bank(4 * s + bt), aT.v(kc, tok), wo_big[bt].v(kc), kc == 0, kc == 15, signal=(kc == 15))
        for bt in range(4):
            cs = (bt * 512, bt * 512 + 512)
            tm = tmpD[bt % 2]
            TT("dve", tm.v(), bank(4 * s + bt), gate_bc.v(cs), ALU.mult)
            TT("dve", xd[s].v(cs), xd[s].v(cs), tm.v(), ALU.add)
        if b >= 1:
            d_finish(b - 1)
    d_finish(NB - 1)

    assert tile_pos[0] == NT_RING and issued[0] == NT_RING, (tile_pos[0], issued[0], NT_RING)


_MASKS = None


def _masks(has_prev):
    j = np.arange(128)[:, None]
    i = np.arange(128)[None, :]
    mp = (j > i).astype(np.float32)
    mc = (j <= i).astype(np.float32)
    m0 = mp * (1.0 if has_prev else 0.0)
    return np.ascontiguousarray(np.stack([m0, mp, mc], axis=1).reshape(128, 3 * 128))


def kernel(x, c, norm_g, w_ada, b_ada, w_in, attn_sinks, sgu_ln_g, sgu_ln_b, sgu_w, sgu_b, w_out, final_g):
    x = np.asarray(x, np.float32)
    B, S, _ = x.shape
    shared = {
        "norm_g": np.ascontiguousarray(np.asarray(norm_g, np.float32)[0]),
        "w_ada": np.ascontiguousarray(np.asarray(w_ada, np.float32)[0]),
        "b_ada": np.ascontiguousarray(np.asarray(b_ada, np.float32)[0]),
        "w_in": np.ascontiguousarray(np.asarray(w_in, np.float32)[0]),
        "sinks": np.ascontiguousarray(np.asarray(attn_sinks, np.float32)[0]),
        "ln_g": np.ascontiguousarray(np.asarray(sgu_ln_g, np.float32)[0]),
        "ln_b": np.ascontiguousarray(np.asarray(sgu_ln_b, np.float32)[0]),
        "sgu_w": np.ascontiguousarray(np.asarray(sgu_w, np.float32)[0]),
        "sgu_b": np.ascontiguousarray(np.asarray(sgu_b, np.float32)[0].reshape(-1)),
        "w_out": np.ascontiguousarray(np.asarray(w_out, np.float32)[0]),
        "final_g": np.ascontiguousarray(np.asarray(final_g, np.float32)),
    }
    c = np.asarray(c, np.float32)
    in_maps = []
    for j in range(NCORES):
        b, half = j // 2, j % 2
        xs = np.zeros((TH, D), np.float32)
        xs[128:] = x[b, half * T:(half + 1) * T]
        if half == 1:
            xs[:128] = x[b, T - 128:T]
        m = dict(shared)
        m["x"] = xs
        m["c"] = np.ascontiguousarray(c[b])
        m["masks"] = _masks(half == 1)
        in_maps.append(m)
    nc = build_nc()
    res = run_bass_kernel_spmd(nc, in_maps, core_ids=list(range(NCORES)))
    out = np.empty((B, S, D), np.float32)
    for j in range(NCORES):
        b, half = j // 2, j % 2
        out[b, half * T:(half + 1) * T] = res.results[j]["y"]
    return out
```
